# Optimizing a Trainium2 kernel written in Bass

```python
import math
import jax, jax.numpy as jnp
from jax import lax
import numpy as np

D_MODEL = 1024
BATCH = 8
SEQ = 2048
DEPTH = 4

N_MIXERS = 3
ATTN_HEADS = 16
ATTN_KV_HEADS = 2
ATTN_GROUP = ATTN_HEADS // ATTN_KV_HEADS
HEAD_DIM = 64
WINDOW = 128
QKV_DIM = (ATTN_HEADS + 2 * ATTN_KV_HEADS) * HEAD_DIM
CONV_WIDTH = 31
HGRN_EXPAND = 128
HGRN_HEADS = D_MODEL // HGRN_EXPAND
HGRN_KDIM = HGRN_EXPAND
HGRN_VDIM = D_MODEL // HGRN_HEADS
FORGET_DIM = HGRN_HEADS * HGRN_KDIM
HGRN_CHUNK = 64
D_FF = 4 * D_MODEL
NORM_EPS = 1e-6
N_ATTN_LAYERS = (DEPTH + 2) // 3
N_CONV_LAYERS = (DEPTH + 1) // 3
N_HGRN_LAYERS = DEPTH // 3

kernel_name = "hybrid_swa_conformer_hgrn2_trunk"


def rms_norm(x, g):
    x32 = x.astype(jnp.float32)
    y = x32 * lax.rsqrt(jnp.mean(x32 * x32, axis=-1, keepdims=True) + NORM_EPS)
    return y.astype(x.dtype) * g


def layer_norm(x, g, b):
    x32 = x.astype(jnp.float32)
    mu = jnp.mean(x32, axis=-1, keepdims=True)
    var = jnp.mean(jnp.square(x32 - mu), axis=-1, keepdims=True)
    y = (x32 - mu) * lax.rsqrt(var + NORM_EPS)
    return y.astype(x.dtype) * g + b


def alibi_slopes(n_heads):
    h = jnp.arange(1, n_heads + 1, dtype=jnp.float32)
    return jnp.exp2(-8.0 * h / n_heads)


def swa_sink_attention(h, w_qkv, sinks, w_o):
    B, T, _ = h.shape
    nb = T // WINDOW
    qkv = h @ w_qkv
    q, k, v = jnp.split(qkv, [ATTN_HEADS * HEAD_DIM, (ATTN_HEADS + ATTN_KV_HEADS) * HEAD_DIM], axis=-1)
    q = q.reshape(B, nb, WINDOW, ATTN_KV_HEADS, ATTN_GROUP, HEAD_DIM)

    def banded(z):
        z = z.reshape(B, T, ATTN_KV_HEADS, HEAD_DIM)
        zp = jnp.pad(z, ((0, 0), (WINDOW, 0), (0, 0), (0, 0)))
        prev = zp[:, :T].reshape(B, nb, WINDOW, ATTN_KV_HEADS, HEAD_DIM)
        cur = zp[:, WINDOW:].reshape(B, nb, WINDOW, ATTN_KV_HEADS, HEAD_DIM)
        return jnp.concatenate([prev, cur], axis=2)

    kb, vb = banded(k), banded(v)
    scale = HEAD_DIM ** -0.5
    scores = jnp.einsum('bnqkgd,bnskd->bkgnqs', q.astype(jnp.float32), kb.astype(jnp.float32)) * scale
    qi = jnp.arange(WINDOW)[:, None]
    si = jnp.arange(2 * WINDOW)[None, :]
    dist = WINDOW + qi - si
    kpos = (jnp.arange(nb)[:, None] - 1) * WINDOW + jnp.arange(2 * WINDOW)[None, :]
    valid = ((dist >= 0) & (dist < WINDOW))[None] & (kpos >= 0)[:, None, :]
    slopes = alibi_slopes(ATTN_HEADS).reshape(ATTN_KV_HEADS, ATTN_GROUP, 1, 1, 1)
    scores = scores - slopes * dist.astype(jnp.float32)
    scores = jnp.where(valid, scores, -1e30)
    sink = jnp.broadcast_to(sinks.astype(jnp.float32).reshape(1, ATTN_KV_HEADS, ATTN_GROUP, 1, 1, 1),
                            scores.shape[:-1] + (1,))
    probs = jax.nn.softmax(jnp.concatenate([scores, sink], axis=-1), axis=-1)[..., :-1]
    out = jnp.einsum('bkgnqs,bnskd->bnqkgd', probs.astype(vb.dtype), vb)
    out = out.reshape(B, T, ATTN_HEADS * HEAD_DIM)
    return out @ w_o


def conformer_conv(h, w_pw1, b_pw1, w_dw, b_dw, ln_g, ln_b, w_pw2, b_pw2):
    u = h @ w_pw1 + b_pw1
    a, gate = jnp.split(u, 2, axis=-1)
    u = a * jax.nn.sigmoid(gate)
    up = jnp.pad(u, ((0, 0), (CONV_WIDTH - 1, 0), (0, 0)))
    c = lax.conv_general_dilated(up, w_dw[:, None, :].astype(up.dtype), window_strides=(1,), padding='VALID',
                                 dimension_numbers=('NWC', 'WIO', 'NWC'), feature_group_count=D_MODEL)
    c = c + b_dw
    c = jax.nn.silu(layer_norm(c, ln_g, ln_b))
    return c @ w_pw2 + b_pw2


def hgrn2(h, w_qfig, lower_bound, norm_g, w_o):
    B, T, _ = h.shape
    nc = T // HGRN_CHUNK
    q, f, i, g = jnp.split(h @ w_qfig, [FORGET_DIM, 2 * FORGET_DIM, 2 * FORGET_DIM + D_MODEL], axis=-1)
    q = jax.nn.silu(q.astype(jnp.float32)) * (HGRN_KDIM ** -0.5)
    f = lower_bound + (1.0 - lower_bound) * jax.nn.sigmoid(f.astype(jnp.float32))
    k = 1.0 - f
    logf = jnp.log(f)

    def to_chunks(z, dim):
        z = z.astype(jnp.float32).reshape(B, nc, HGRN_CHUNK, HGRN_HEADS, dim)
        return z.transpose(1, 0, 3, 2, 4)

    qc, kc, gc = to_chunks(q, HGRN_KDIM), to_chunks(k, HGRN_KDIM), to_chunks(logf, HGRN_KDIM)
    vc = to_chunks(i, HGRN_VDIM)
    causal = jnp.tril(jnp.ones((HGRN_CHUNK, HGRN_CHUNK), dtype=bool))[:, :, None]

    def step(S, inp):
        q_, k_, v_, g_ = inp
        b = jnp.cumsum(g_, axis=2)
        diff = b[:, :, :, None, :] - b[:, :, None, :, :]
        decay = jnp.exp(jnp.where(causal, diff, -jnp.inf))
        A = jnp.einsum('bhtk,bhsk,bhtsk->bhts', q_, k_, decay)
        o = jnp.einsum('bhts,bhsv->bhtv', A, v_) + jnp.einsum('bhtk,bhkv->bhtv', q_ * jnp.exp(b), S)
        b_last = b[:, :, -1:, :]
        S_new = jnp.exp(b_last[:, :, 0, :])[..., None] * S + jnp.einsum('bhsk,bhsv->bhkv', k_ * jnp.exp(b_last - b), v_)
        return S_new, o

    S0 = jnp.zeros((B, HGRN_HEADS, HGRN_KDIM, HGRN_VDIM), jnp.float32)
    _, o = lax.scan(step, S0, (qc, kc, vc, gc))
    o = o.transpose(1, 0, 3, 2, 4).reshape(B, T, HGRN_HEADS, HGRN_VDIM)
    o = o * lax.rsqrt(jnp.mean(o * o, axis=-1, keepdims=True) + NORM_EPS)
    o = o.reshape(B, T, D_MODEL).astype(h.dtype) * norm_g * jax.nn.silu(g)
    return o @ w_o


def squared_relu_mlp(h, w1, w2):
    return jnp.square(jax.nn.relu(h @ w1)) @ w2


def setup_inputs(seed: int = 0) -> dict:
    key = jax.random.key(seed)
    ks = jax.random.split(key, 24)
    nrm = lambda k, shape, s: jax.random.normal(k, shape, jnp.float32) * s
    D = D_MODEL
    return {
        "x": nrm(ks[0], (BATCH, SEQ, D), 1.0),
        "attn_w_qkv": nrm(ks[1], (N_ATTN_LAYERS, D, QKV_DIM), D ** -0.5),
        "attn_sinks": nrm(ks[2], (N_ATTN_LAYERS, ATTN_HEADS), 0.5),
        "attn_w_o": nrm(ks[3], (N_ATTN_LAYERS, ATTN_HEADS * HEAD_DIM, D), (ATTN_HEADS * HEAD_DIM) ** -0.5),
        "conv_w_pw1": nrm(ks[4], (N_CONV_LAYERS, D, 2 * D), D ** -0.5),
        "conv_b_pw1": nrm(ks[5], (N_CONV_LAYERS, 2 * D), 0.02),
        "conv_w_dw": nrm(ks[6], (N_CONV_LAYERS, CONV_WIDTH, D), CONV_WIDTH ** -0.5),
        "conv_b_dw": nrm(ks[7], (N_CONV_LAYERS, D), 0.02),
        "conv_ln_g": 1.0 + nrm(ks[8], (N_CONV_LAYERS, D), 0.02),
        "conv_ln_b": nrm(ks[9], (N_CONV_LAYERS, D), 0.02),
        "conv_w_pw2": nrm(ks[10], (N_CONV_LAYERS, D, D), D ** -0.5),
        "conv_b_pw2": nrm(ks[11], (N_CONV_LAYERS, D), 0.02),
        "hgrn_w_qfig": nrm(ks[12], (N_HGRN_LAYERS, D, 2 * FORGET_DIM + 2 * D), D ** -0.5),
        "hgrn_lower_bounds": nrm(ks[13], (DEPTH, FORGET_DIM), 0.1),
        "hgrn_norm_g": 1.0 + nrm(ks[14], (N_HGRN_LAYERS, D), 0.02),
        "hgrn_w_o": nrm(ks[15], (N_HGRN_LAYERS, D, D), D ** -0.5),
        "norm_mixer": 1.0 + nrm(ks[16], (DEPTH, D), 0.02),
        "norm_mlp": 1.0 + nrm(ks[17], (DEPTH, D), 0.02),
        "mlp_w1": nrm(ks[18], (DEPTH, D, D_FF), D ** -0.5),
        "mlp_w2": nrm(ks[19], (DEPTH, D_FF, D), D_FF ** -0.5),
        "final_norm": 1.0 + nrm(ks[20], (D,), 0.02),
    }


def reference(x, attn_w_qkv, attn_sinks, attn_w_o, conv_w_pw1, conv_b_pw1, conv_w_dw, conv_b_dw,
              conv_ln_g, conv_ln_b, conv_w_pw2, conv_b_pw2, hgrn_w_qfig, hgrn_lower_bounds, hgrn_norm_g,
              hgrn_w_o, norm_mixer, norm_mlp, mlp_w1, mlp_w2, final_norm):
    lb = jax.nn.softmax(hgrn_lower_bounds.astype(jnp.float32), axis=0)
    lb = jnp.cumsum(lb, axis=0) - lb[0:1]
    h = x
    for i in range(DEPTH):
        kind = i % N_MIXERS
        j = i // N_MIXERS
        hn = rms_norm(h, norm_mixer[i])
        if kind == 0:
            y = swa_sink_attention(hn, attn_w_qkv[j], attn_sinks[j], attn_w_o[j])
        elif kind == 1:
            y = conformer_conv(hn, conv_w_pw1[j], conv_b_pw1[j], conv_w_dw[j], conv_b_dw[j],
                               conv_ln_g[j], conv_ln_b[j], conv_w_pw2[j], conv_b_pw2[j])
        else:
            y = hgrn2(hn, hgrn_w_qfig[j], lb[i], hgrn_norm_g[j], hgrn_w_o[j])
        h = h + y
        h = h + squared_relu_mlp(rms_norm(h, norm_mlp[i]), mlp_w1[i], mlp_w2[i])
    return rms_norm(h, final_norm)
```

```python
import math
from contextlib import ExitStack

import numpy as np
import concourse.bass as bass
import concourse.mybir as mybir
from concourse.bass_utils import run_bass_kernel_spmd

F32 = mybir.dt.float32
BF16 = mybir.dt.bfloat16
AF = mybir.ActivationFunctionType
ALU = mybir.AluOpType

D = 1024
T = 2048
NCH = 8
TB = 512
NTB = T // TB
DFF = 4096
DEPTH = 4
EPS = 1e-6
N_CORES = 8

PV = {}
_pvn = 0
for _name, _n in (("gains", 9 * NCH), ("sinks", 2 * NCH), ("cb1", 16), ("cdw", 31 * NCH), ("cbd", NCH),
                  ("clg", NCH), ("clb", NCH), ("cb2", NCH), ("hlb", 4 * NCH), ("hng", NCH)):
    PV[_name] = (_pvn, _pvn + _n)
    _pvn += _n
PV_N = _pvn

_ESZ = {F32: 4, BF16: 2}


def _esz(dt):
    return _ESZ.get(dt, 4)


def _region(ap):
    esz = _esz(ap.dtype)
    dims = [tuple(d) for d in ap.ap]
    rowstride, npart = dims[0]
    if rowstride == 0:
        rowstride = 1 << 40
    p_lo = ap.offset // rowstride
    off = ap.offset % rowstride
    ivs = [(0, 1)]
    for step, cnt in reversed(dims[1:]):
        if cnt <= 1 or step == 0:
            continue
        if len(ivs) == 1 and step == ivs[0][1] - ivs[0][0]:
            ivs = [(ivs[0][0], ivs[0][0] + step * cnt)]
        elif len(ivs) * cnt <= 64 and step > 0:
            ivs = [(lo + i * step, hi + i * step) for i in range(cnt) for (lo, hi) in ivs]
        else:
            lo = min(a for a, _ in ivs)
            hi = max(b for _, b in ivs)
            if step > 0:
                ivs = [(lo, hi + step * (cnt - 1))]
            else:
                ivs = [(lo + step * (cnt - 1), hi)]
    name = ap.tensor.name
    if name.startswith("ps"):
        return (name, (p_lo // 32) * 32, -(-(p_lo + npart) // 32) * 32, [(0, 2048)])
    return (name, p_lo, p_lo + npart, [((off + a) * esz, (off + b) * esz) for a, b in ivs])


class Sched:
    COMPUTE = ("pe", "act", "dve", "pool")
    SIG_WRAP = 12000

    def __init__(self):
        self.ops = {e: [] for e in ("pe", "act", "dve", "pool", "sp")}
        self.recs = {}
        self.seen = {e: {} for e in self.ops}
        self.dma_count = {}
        self.dma_sems = []

    @staticmethod
    def _overlap(r, p_lo, p_hi, ivs):
        if r["p_hi"] <= p_lo or p_hi <= r["p_lo"]:
            return 0
        full = True
        anyov = False
        for (a, b) in r["ivs"]:
            cov = False
            for (lo, hi) in ivs:
                if a < hi and lo < b:
                    anyov = True
                    if lo <= a and b <= hi:
                        cov = True
            if not cov:
                full = False
        if not anyov:
            return 0
        if full and p_lo <= r["p_lo"] and r["p_hi"] <= p_hi:
            return 2
        return 1

    @staticmethod
    def _isub(A, B):
        out = []
        for (a, b) in A:
            cur = [(a, b)]
            for (lo, hi) in B:
                nxt = []
                for (x, y) in cur:
                    if hi <= x or y <= lo:
                        nxt.append((x, y))
                    else:
                        if x < lo:
                            nxt.append((x, lo))
                        if hi < y:
                            nxt.append((hi, y))
                cur = nxt
            out.extend(cur)
        return out

    @staticmethod
    def _iand(A, B):
        out = []
        for (a, b) in A:
            for (lo, hi) in B:
                x, y = max(a, lo), min(b, hi)
                if x < y:
                    out.append((x, y))
        return out

    def _split(self, r, p_lo, p_hi, ivs):
        outs = []
        if r["p_lo"] < p_lo:
            outs.append({"p_lo": r["p_lo"], "p_hi": p_lo, "ivs": r["ivs"], "w": list(r["w"]), "r": list(r["r"])})
        if p_hi < r["p_hi"]:
            outs.append({"p_lo": p_hi, "p_hi": r["p_hi"], "ivs": r["ivs"], "w": list(r["w"]), "r": list(r["r"])})
        q_lo, q_hi = max(p_lo, r["p_lo"]), min(p_hi, r["p_hi"])
        rest = self._isub(r["ivs"], ivs)
        if rest:
            outs.append({"p_lo": q_lo, "p_hi": q_hi, "ivs": rest, "w": list(r["w"]), "r": list(r["r"])})
        inside = {"p_lo": q_lo, "p_hi": q_hi, "ivs": self._iand(r["ivs"], ivs), "w": list(r["w"]), "r": list(r["r"])}
        return inside, outs

    def _deps(self, reads, writes, tok):
        deps = []
        rregs = [_region(a) for a in reads]
        wregs = [_region(a) for a in writes]
        for (sp, p_lo, p_hi, ivs) in rregs:
            lst = self.recs.get(sp, [])
            keep = []
            for r in lst:
                ov = self._overlap(r, p_lo, p_hi, ivs)
                if not ov:
                    keep.append(r)
                    continue
                deps.extend(r["w"])
                if sp.startswith("ps"):
                    deps.extend(t for t in r["r"] if t[0] == "c" and t[1] != tok[1])
                if ov == 2:
                    inside, outs = r, []
                else:
                    inside, outs = self._split(r, p_lo, p_hi, ivs)
                if tok[0] == "c":
                    inside["r"] = [t for t in inside["r"] if not (t[0] == "c" and t[1] == tok[1])]
                inside["r"].append(tok)
                keep.append(inside)
                keep.extend(outs)
            self.recs[sp] = keep
        for (sp, p_lo, p_hi, ivs) in wregs:
            lst = self.recs.setdefault(sp, [])
            keep = []
            for r in lst:
                ov = self._overlap(r, p_lo, p_hi, ivs)
                if not ov:
                    keep.append(r)
                    continue
                deps.extend(t for t in r["w"] if t != tok)
                deps.extend(t for t in r["r"] if t != tok)
                if ov == 1:
                    _, outs = self._split(r, p_lo, p_hi, ivs)
                    keep.extend(outs)
            keep.append({"p_lo": p_lo, "p_hi": p_hi, "ivs": ivs, "w": [tok], "r": []})
            self.recs[sp] = keep
        return deps

    def _plan_waits(self, eng, deps, extra=()):
        need = {}
        for t in list(deps) + list(extra):
            if t is None:
                continue
            key = (t[0], t[1])
            if t[0] == "c" and t[1] == eng and eng == "pe":
                continue
            need[key] = max(need.get(key, -1), t[2])
        waits = []
        seen = self.seen[eng]
        for key, v in need.items():
            if seen.get(key, -1) >= v:
                continue
            seen[key] = v
            waits.append((key[0], key[1], v))
            if key[0] == "c":
                self.ops[key[1]][v]["sig"] = True
        return waits

    def op(self, eng, fn, reads=(), writes=(), extra_waits=()):
        idx = len(self.ops[eng])
        tok = ("c", eng, idx)
        deps = self._deps(reads, writes, tok)
        waits = self._plan_waits(eng, deps, extra_waits)
        self.ops[eng].append({"fn": fn, "waits": waits, "sig": False, "dma": None})
        return tok

    def dma(self, queue, semkey, pairs, reads=(), writes=(), extra_waits=()):
        if semkey not in self.dma_count:
            self.dma_count[semkey] = 0
            self.dma_sems.append(semkey)
        self.dma_count[semkey] += 16 * len(pairs)
        tok = ("d", semkey, self.dma_count[semkey])
        deps = self._deps(reads, writes, tok)
        waits = self._plan_waits(queue, deps, extra_waits)
        for i, (o, s) in enumerate(pairs):
            self.ops[queue].append({"fn": None, "waits": waits if i == 0 else [], "sig": False,
                                    "dma": (o, s, semkey)})
        return tok

    def wait_all_dma(self, eng, semkeys):
        toks = [("d", k, self.dma_count[k]) for k in semkeys]
        waits = self._plan_waits(eng, toks)
        self.ops[eng].append({"fn": None, "waits": waits, "sig": False, "dma": None})

    def emit(self, nc, es):
        nsig = {e: sum(1 for o in self.ops[e] if o["sig"]) for e in self.COMPUTE}
        esems = {}
        for e in self.COMPUTE:
            n = max(1, -(-nsig[e] // self.SIG_WRAP))
            esems[e] = [es.enter_context(nc.semaphore(f"s_{e}{i}")) for i in range(n)]
        dsems = {k: es.enter_context(nc.semaphore(f"d_{k}")) for k in self.dma_sems}
        rank = {}
        for e in self.COMPUTE:
            r = 0
            for i, o in enumerate(self.ops[e]):
                if o["sig"]:
                    rank[(e, i)] = r
                    r += 1
        W = self.SIG_WRAP

        def replay(ename, eng):
            for i, o in enumerate(self.ops[ename]):
                for (kind, key, v) in o["waits"]:
                    if kind == "c":
                        r = rank[(key, v)]
                        eng.wait_ge(esems[key][r // W], r % W + 1)
                    else:
                        eng.wait_ge(dsems[key], v)
                if o["dma"] is not None:
                    out, src, semkey = o["dma"]
                    eng.dma_start(out=out, in_=src).then_inc(dsems[semkey], 16)
                elif o["fn"] is not None:
                    ins = o["fn"](eng)
                    if o["sig"]:
                        r = rank[(ename, i)]
                        ins.then_inc(esems[ename][r // W], 1)

        block = es.enter_context(nc.Block())

        @block.tensor
        def _(e):
            replay("pe", e)

        @block.scalar
        def _(e):
            replay("act", e)

        @block.vector
        def _(e):
            replay("dve", e)

        @block.gpsimd
        def _(e):
            replay("pool", e)

        @block.sync
        def _(e):
            replay("sp", e)


LAYER_KIND = [0, 1, 2, 0]
LAYER_J = [0, 0, 0, 1]
MIXER_W = {
    0: [("attn_w_qkv", (D, 1280)), ("attn_w_o", (D, D))],
    1: [("conv_w_pw1", (D, 2 * D)), ("conv_w_pw2", (D, D))],
    2: [("hgrn_w_qfig", (D, 4 * D)), ("hgrn_w_o", (D, D))],
}
HEADS = 16
HD = 64
SLOPES = [2.0 ** (-8.0 * (i + 1) / HEADS) for i in range(HEADS)]


class Builder:
    def __init__(self, layers=(0, 1, 2, 3), final_norm=True, mixers=True, mlps=True):
        self.layers = list(layers)
        self.final_norm = final_norm
        self.do_mixers = mixers
        self.do_mlps = mlps
        self.nc = bass.Bass("TRN2", target_bir_lowering=False)
        self.S = Sched()
        self.es = ExitStack()
        self._psrr = 0

    def dram_in(self, name, shape):
        return self.nc.dram_tensor(name, list(shape), F32, kind="ExternalInput").ap()

    def sb(self, name, shape, dt):
        return self.es.enter_context(self.nc.sbuf_tensor(name, list(shape), dt))

    def declare(self):
        nc = self.nc
        self.xT = self.dram_in("xT", (D, T))
        self.outT = nc.dram_tensor("outT", [D, T], F32, kind="ExternalOutput").ap()
        self.w = {}
        self.pvec_d = self.dram_in("pvec", (128, PV_N))
        for li in self.layers:
            if self.do_mlps:
                self.w[f"mlp_w1_{li}"] = self.dram_in(f"mlp_w1_{li}", (D, DFF))
                self.w[f"mlp_w2_{li}"] = self.dram_in(f"mlp_w2_{li}", (DFF, D))
            if self.do_mixers:
                for nm, shp in MIXER_W[LAYER_KIND[li]]:
                    self.w[f"{nm}_{li}"] = self.dram_in(f"{nm}_{li}", shp)

        self.h = self.sb("h", (128, NCH, T), F32)
        self.xn = self.sb("xn", (128, NCH, T), BF16)
        self.ARENA_F32 = 26112
        self.arena = self.sb("arena", (128, self.ARENA_F32), F32)
        self.ones_bf = self.sb("ones_bf", (128, 128), BF16)
        self.pvec = self.sb("pvec_sb", (128, PV_N), F32)
        self.gvec = self.pvec[:, PV["gains"][0]:PV["gains"][1]].rearrange("p (i c) -> p i c", c=NCH)
        self.ps = [self.es.enter_context(nc.psum_tensor(f"ps{i}", [128, 512], F32)) for i in range(8)]

    def carve(self, off_bytes, shape, dt):
        n = int(np.prod(shape[1:]))
        esz = _esz(dt)
        assert off_bytes % 4 == 0 and (n * esz) % 4 == 0
        a = off_bytes // 4
        b = a + n * esz // 4
        assert b <= self.ARENA_F32, (off_bytes, shape)
        v = self.arena[:, a:b]
        if dt != F32:
            v = v.bitcast(dt)
        if len(shape) == 3:
            v = v.rearrange("p (a b) -> p a b", a=shape[1])
        elif len(shape) == 4:
            v = v.rearrange("p (a b c) -> p a b c", a=shape[1], b=shape[2])
        return v

    def setup(self):
        S = self.S
        S.op("pool", lambda e: e.memset(self.ones_bf[:], 1.0), writes=[self.ones_bf[:]])
        S.dma("sp", "pv", [(self.pvec[:], self.pvec_d[:, :])], writes=[self.pvec[:]])
        xv = self.xT.rearrange("(c p) t -> p c t", p=128)
        for tb in range(NTB):
            sl = slice(tb * TB, (tb + 1) * TB)
            S.dma("sp", f"x{tb}", [(self.h[:, :, sl], xv[:, :, sl])], writes=[self.h[:, :, sl]])

    def _small_dma(self, pairs, semkey, writes):
        nc = self.nc
        S = self.S
        idx0 = len(S.ops["sp"])
        S.dma("sp", semkey, pairs, writes=writes)
        for o in S.ops["sp"][idx0:]:
            o["slow"] = True

    def psbank(self):
        i = self._psrr
        self._psrr = (self._psrr + 1) % 8
        return self.ps[i]

    def rmsnorm(self, gi, out_fn, scratch_off=0):
        S = self.S
        sq = [self.carve(scratch_off + i * 8192, (128, NCH, TB), BF16) for i in range(3)]
        rs = [self.carve(scratch_off + 24576 + i * 2048, (128, TB), F32) for i in range(2)]

        def square(tb):
            sl = slice(tb * TB, (tb + 1) * TB)
            sqt = sq[tb % 3]
            if tb % 2 == 0:
                S.op("act", lambda e: e.activation(out=sqt[:], in_=self.h[:, :, sl], func=AF.Square),
                     reads=[self.h[:, :, sl]], writes=[sqt[:]])
            else:
                S.op("pool", lambda e: e.tensor_tensor(out=sqt[:], in0=self.h[:, :, sl], in1=self.h[:, :, sl], op=ALU.mult),
                     reads=[self.h[:, :, sl]], writes=[sqt[:]])

        square(0)
        square(1)
        for tb in range(NTB):
            sl = slice(tb * TB, (tb + 1) * TB)
            if tb + 2 < NTB:
                square(tb + 2)
            sqt = sq[tb % 3]
            ps = self.psbank()
            for k in range(NCH):
                S.op("pe", lambda e, ps=ps, sqt=sqt, k=k: e.matmul(ps[:], lhsT=self.ones_bf[:], rhs=sqt[:, k, :],
                                                                     start=(k == 0), stop=(k == NCH - 1)),
                     reads=[self.ones_bf[:], sqt[:, k, :]], writes=[ps[:]])
            r = rs[tb % 2]
            S.op("act", lambda e, ps=ps, r=r: e.activation(out=r[:], in_=ps[:], func=AF.Ln, scale=1.0 / D, bias=self.eps_t[:, 0:1]),
                 reads=[ps[:], self.eps_t[:]], writes=[r[:]])
            S.op("act", lambda e, r=r: e.activation(out=r[:], in_=r[:], func=AF.Exp, scale=-0.5),
                 reads=[r[:]], writes=[r[:]])
            out_fn(tb, sl, r)

    def norm_to_xn(self, gi, scratch_off=0):
        S = self.S

        def out_fn(tb, sl, r):
            for k in range(NCH):
                S.op("dve", lambda e, k=k, sl=sl, r=r: e.scalar_tensor_tensor(
                        out=self.xn[:, k, sl], in0=self.h[:, k, sl], scalar=self.gvec[:, gi, k:k + 1], in1=r[:],
                        op0=ALU.mult, op1=ALU.mult),
                     reads=[self.h[:, k, sl], self.gvec[:, gi, k:k + 1], r[:]], writes=[self.xn[:, k, sl]])
        self.rmsnorm(gi, out_fn, scratch_off)

    def final(self):
        S = self.S
        ov = self.outT.rearrange("(c p) t -> p c t", p=128)

        def out_fn(tb, sl, r):
            for k in range(NCH):
                S.op("dve", lambda e, k=k, sl=sl, r=r: e.scalar_tensor_tensor(
                        out=self.h[:, k, sl], in0=self.h[:, k, sl], scalar=self.gvec[:, 8, k:k + 1], in1=r[:],
                        op0=ALU.mult, op1=ALU.mult),
                     reads=[self.h[:, k, sl], self.gvec[:, 8, k:k + 1], r[:]], writes=[self.h[:, k, sl]])
            S.dma("sp", f"o{tb}", [(ov[:, :, sl], self.h[:, :, sl])], reads=[self.h[:, :, sl]])
        self.rmsnorm(8, out_fn, 0)
        S.wait_all_dma("sp", [f"o{tb}" for tb in range(NTB)])

    def store_h(self):
        S = self.S
        ov = self.outT.rearrange("(c p) t -> p c t", p=128)
        for tb in range(NTB):
            sl = slice(tb * TB, (tb + 1) * TB)
            S.dma("sp", f"o{tb}", [(ov[:, :, sl], self.h[:, :, sl])], reads=[self.h[:, :, sl]])
        S.wait_all_dma("sp", [f"o{tb}" for tb in range(NTB)])

    def attention(self, li):
        S = self.S
        aj = LAYER_J[li]
        QT_OFF, KT_OFF, V_OFF, DM_OFF, WQ_OFF, WO_OFF, EX_OFF, PT_OFF, RD_OFF = (
            0, 32768, 40960, 45056, 53248, 75776, 92160, 96256, 100352)
        qT = self.carve(QT_OFF, (128, NCH, T), BF16)
        kT = self.carve(KT_OFF, (128, 2, T), BF16)
        vtm = self.carve(V_OFF, (128, 16, 128), BF16)
        DM = self.carve(DM_OFF, (128, HEADS, 256), BF16)
        wq = self.carve(WQ_OFF, (128, NCH, 1408), BF16)
        wo = self.carve(WO_OFF, (128, NCH, D), BF16)
        exs = [self.carve(EX_OFF + i * 2048, (128, 512), F32) for i in range(2)]
        pts = [self.carve(PT_OFF + i * 1024, (128, 512), BF16) for i in range(2)]
        rds = [self.carve(RD_OFF + i * 2048, (128, 512), F32) for i in range(2)]
        wqd = self.w[f"attn_w_qkv_{li}"].rearrange("(k p) f -> p k f", p=128)
        wod = self.w[f"attn_w_o_{li}"].rearrange("(k p) f -> p k f", p=128)
        S.dma("pool", "aq", [(wq[:, :, 0:1024], wqd[:, :, 0:1024]),
                             (wq[:, :, 1024:1088], wqd[:, :, 1024:1088]), (wq[:, :, 1088:1152], wqd[:, :, 1024:1088]),
                             (wq[:, :, 1152:1216], wqd[:, :, 1088:1152]), (wq[:, :, 1216:1280], wqd[:, :, 1088:1152]),
                             (wq[:, :, 1280:1408], wqd[:, :, 1152:1280])], writes=[wq[:]])
        S.dma("pool", "ao", [(wo[:], wod)], writes=[wo[:]])
        self.norm_to_xn(li, scratch_off=QT_OFF)
        import os
        stop = int(os.environ.get("ATT_STOP", "9"))
        if stop <= 1:
            return

        di = self.iota_i
        df = self.carve(EX_OFF + 1024, (128, 256), F32)
        m1 = self.carve(EX_OFF + 2048, (128, 256), F32)
        m2 = self.carve(EX_OFF + 3072, (128, 256), F32)
        tmp = [self.carve(PT_OFF + i * 1024, (128, 256), F32) for i in range(2)]
        S.op("pool", lambda e: e.iota(di[:], pattern=[[-128, 2], [1, 128]], base=128, channel_multiplier=-1), writes=[di[:]])
        S.op("dve", lambda e: e.tensor_copy(out=df[:], in_=di[:]), reads=[di[:]], writes=[df[:]])
        S.op("dve", lambda e: e.tensor_single_scalar(out=m1[:], in_=df[:], scalar=0.0, op=ALU.is_ge), reads=[df[:]], writes=[m1[:]])
        S.op("dve", lambda e: e.tensor_single_scalar(out=m2[:], in_=df[:], scalar=127.0, op=ALU.is_le), reads=[df[:]], writes=[m2[:]])
        S.op("dve", lambda e: e.tensor_tensor(out=m1[:], in0=m1[:], in1=m2[:], op=ALU.mult), reads=[m1[:], m2[:]], writes=[m1[:]])
        S.op("dve", lambda e: e.tensor_scalar(out=df[:], in0=df[:], scalar1=0.0, scalar2=128.0, op0=ALU.max, op1=ALU.min),
             reads=[df[:]], writes=[df[:]])
        for hd in range(HEADS):
            t = tmp[hd % 2]
            S.op("act", lambda e, t=t, hd=hd: e.activation(out=t[:], in_=df[:], func=AF.Exp, scale=-SLOPES[hd]),
                 reads=[df[:]], writes=[t[:]])
            S.op("dve", lambda e, t=t, hd=hd: e.tensor_tensor(out=DM[:, hd, :], in0=t[:], in1=m1[:], op=ALU.mult),
                 reads=[t[:], m1[:]], writes=[DM[:, hd, :]])
        es = self.es_t[:, aj * NCH:(aj + 1) * NCH]
        c0 = PV["sinks"][0] + aj * NCH
        S.op("act", lambda e: e.activation(out=es, in_=self.pvec[:, c0:c0 + NCH], func=AF.Exp),
             reads=[self.pvec[:, c0:c0 + NCH]], writes=[es])

        if stop <= 2:
            return
        cnt = [0]

        def evac(dst, ps_ap):
            if cnt[0] % 2 == 0:
                S.op("act", lambda e: e.activation(out=dst, in_=ps_ap, func=AF.Copy), reads=[ps_ap], writes=[dst])
            else:
                S.op("dve", lambda e: e.tensor_copy(out=dst, in_=ps_ap), reads=[ps_ap], writes=[dst])
            cnt[0] += 1

        for cj in range(NCH + 2):
            for tb in range(NTB):
                sl = slice(tb * TB, (tb + 1) * TB)
                ps = self.psbank()
                for k in range(NCH):
                    S.op("pe", lambda e, ps=ps, k=k, cj=cj, sl=sl: e.matmul(
                            ps[:], lhsT=wq[:, k, cj * 128:(cj + 1) * 128], rhs=self.xn[:, k, sl],
                            start=(k == 0), stop=(k == NCH - 1)),
                         reads=[wq[:, k, cj * 128:(cj + 1) * 128], self.xn[:, k, sl]], writes=[ps[:]])
                dst = qT[:, cj, sl] if cj < NCH else kT[:, cj - NCH, sl]
                evac(dst, ps[:])
        for g in range(4):
            ps = self.psbank()
            for ii in range(4):
                i = g * 4 + ii
                for k in range(NCH):
                    S.op("pe", lambda e, ps=ps, k=k, i=i, ii=ii: e.matmul(
                            ps[:, ii * 128:(ii + 1) * 128], lhsT=self.xn[:, k, i * 128:(i + 1) * 128], rhs=wq[:, k, 1280:1408],
                            start=(k == 0), stop=(k == NCH - 1)),
                         reads=[self.xn[:, k, i * 128:(i + 1) * 128], wq[:, k, 1280:1408]], writes=[ps[:, ii * 128:(ii + 1) * 128]])
            evac(vtm[:, g * 4:(g + 1) * 4, :], ps[:].rearrange("p (a b) -> p a b", a=4))

        if stop <= 3:
            return
        aoT = self.xn
        core = int(os.environ.get("ATT_CORE", "9"))
        items = [(j, q) for j in range(NCH) for q in range(8)]
        sbanks = [[self.ps[0], self.ps[1]], [self.ps[2], self.ps[3]]]
        nbanks = [self.ps[4], self.ps[5]]
        dbanks = [self.ps[6], self.ps[7]]
        exs = [[self.carve(EX_OFF + (i * 2 + hh) * 1024, (128, 512), BF16) for hh in range(2)] for i in range(2)]
        pts = [[self.carve(PT_OFF + (i * 2 + hh) * 1024, (128, 512), BF16) for hh in range(2)] for i in range(2)]

        def v4(ap):
            return ap.rearrange("p (a b c) -> p a b c", a=2, b=2)

        def scores(ix):
            j, q = items[ix]
            kv = j // 4
            for hh in range(2):
                ps = sbanks[ix % 2][hh]
                psv = v4(ps[:])
                pr = slice(hh * 64, (hh + 1) * 64)
                for nn in range(2):
                    n = 2 * q + nn
                    for mb in range(2):
                        m = max(n - 1 + mb, 0)
                        S.op("pe", lambda e, psv=psv, nn=nn, mb=mb, m=m, pr=pr, n=n: e.matmul(
                                psv[:, nn, mb, :], lhsT=kT[pr, kv, m * 128:(m + 1) * 128], rhs=qT[pr, j, n * 128:(n + 1) * 128],
                                start=True, stop=True),
                             reads=[kT[pr, kv, m * 128:(m + 1) * 128], qT[pr, j, n * 128:(n + 1) * 128]],
                             writes=[psv[:, nn, mb, :]])
                if core == 0:
                    continue
                ex = exs[ix % 2][hh]
                pt = pts[ix % 2][hh]
                S.op("act", lambda e, ex=ex, ps=ps: e.activation(out=ex[:], in_=ps[:], func=AF.Exp, scale=HD ** -0.5),
                     reads=[ps[:]], writes=[ex[:]])
                for nn in range(2):
                    cs = slice(nn * 256, (nn + 1) * 256)
                    S.op("pool" if hh == 0 else "dve", lambda e, ex=ex, pt=pt, cs=cs, hh=hh: e.tensor_tensor(out=pt[:, cs], in0=ex[:, cs], in1=DM[:, 2 * j + hh, :], op=ALU.mult),
                         reads=[ex[:, cs], DM[:, 2 * j + hh, :]], writes=[pt[:, cs]])

        def pv(ix):
            j, q = items[ix]
            kv = j // 4
            g = ix // 2
            psN = nbanks[g % 2]
            psD = dbanks[g % 2]
            for nn in range(2):
                n = 2 * q + nn
                cs = slice((n % 4) * 128, (n % 4 + 1) * 128)
                for hh in range(2):
                    ptv = v4(pts[ix % 2][hh][:])
                    pr = slice(hh * 64, (hh + 1) * 64)
                    mlist = [mb for mb in range(2) if n - 1 + mb >= 0]
                    for (dst, which) in ((psN, 0), (psD, 1)):
                        for mb in mlist:
                            m = n - 1 + mb
                            lhs = vtm[:, m, kv * 64:(kv + 1) * 64] if which == 0 else self.ones_bf[:, 0:64]
                            S.op("pe", lambda e, dst=dst, pr=pr, cs=cs, lhs=lhs, ptv=ptv, nn=nn, mb=mb, mlist=mlist: e.matmul(
                                    dst[pr, cs], lhsT=lhs, rhs=ptv[:, nn, mb, :],
                                    start=(mb == mlist[0]), stop=(mb == mlist[-1])),
                                 reads=[lhs, ptv[:, nn, mb, :]], writes=[dst[pr, cs]])
        def normalize(ix):
            j, q = items[ix]
            g = ix // 2
            psN = nbanks[g % 2]
            psD = dbanks[g % 2]
            if True:
                rd = rds[g % 2]
                osl = slice((q - 1) * 256, (q + 1) * 256)
                S.op("act", lambda e: e.activation(out=rd[:], in_=psD[:], func=AF.Ln, bias=es[:, j:j + 1]),
                     reads=[psD[:], es[:, j:j + 1]], writes=[rd[:]])
                S.op("act", lambda e: e.activation(out=rd[:], in_=rd[:], func=AF.Exp, scale=-1.0), reads=[rd[:]], writes=[rd[:]])
                S.op("dve", lambda e: e.tensor_tensor(out=aoT[:, j, osl], in0=psN[:], in1=rd[:], op=ALU.mult),
                     reads=[psN[:], rd[:]], writes=[aoT[:, j, osl]])

        scores(0)
        for ix in range(len(items)):
            if ix + 1 < len(items):
                scores(ix + 1)
            if core >= 2:
                pv(ix)
                if ix >= 1 and (ix - 1) % 2 == 1:
                    normalize(ix - 1)
        normalize(len(items) - 1)

        if stop <= 4:
            return
        for tb in range(NTB):
            for m in range(NCH):
                sl = slice(tb * TB, (tb + 1) * TB)
                ps = self.ps[(tb * NCH + m) % 8]
                for j in range(NCH):
                    S.op("pe", lambda e, ps=ps, m=m, j=j, sl=sl: e.matmul(
                            ps[:], lhsT=wo[:, j, m * 128:(m + 1) * 128], rhs=aoT[:, j, sl],
                            start=(j == 0), stop=(j == NCH - 1)),
                         reads=[wo[:, j, m * 128:(m + 1) * 128], aoT[:, j, sl]], writes=[ps[:]])
                S.op("dve", lambda e, ps=ps, m=m, sl=sl: e.tensor_tensor(out=self.h[:, m, sl], in0=self.h[:, m, sl], in1=ps[:], op=ALU.add),
                     reads=[self.h[:, m, sl], ps[:]], writes=[self.h[:, m, sl]])

    def conv(self, li):
        S = self.S
        PAD = 32
        U_OFF, W1_OFF, W2_OFF, DG_OFF, SG_OFF = 0, 33280, 66048, 82432, 98304
        ub = self.carve(U_OFF, (128, NCH, T + PAD), BF16)
        w1c = self.carve(W1_OFF, (128, NCH, 2 * D), BF16)
        cbf = self.carve(W1_OFF, (128, NCH, T), BF16)
        w2c = self.carve(W2_OFF, (128, NCH, D), BF16)
        dg = [self.carve(DG_OFF + i * 256, (128, 128), BF16) for i in range(62)]
        sg = [self.carve(SG_OFF + i * 2048, (128, TB), F32) for i in range(2)]
        pvc = lambda nm, i: self.pvec[:, PV[nm][0] + i:PV[nm][0] + i + 1]
        S.dma("pool", "cw1", [(w1c[:], self.w[f"conv_w_pw1_{li}"].rearrange("(k p) f -> p k f", p=128))], writes=[w1c[:]])
        S.dma("pool", "cw2", [(w2c[:], self.w[f"conv_w_pw2_{li}"].rearrange("(k p) f -> p k f", p=128))], writes=[w2c[:]])
        self.norm_to_xn(li, scratch_off=U_OFF)
        S.op("pool", lambda e: e.memset(ub[:, :, 0:PAD], 0.0), writes=[ub[:, :, 0:PAD]])
        import os
        stop = int(os.environ.get("CONV_STOP", "9"))
        if stop <= 1:
            return
        ix = 0
        for j in range(NCH):
            for tb in range(NTB):
                sl = slice(tb * TB, (tb + 1) * TB)
                psA = self.ps[(2 * ix) % 8]
                psG = self.ps[(2 * ix + 1) % 8]
                for (ps, c0) in ((psG, D + j * 128), (psA, j * 128)):
                    for k in range(NCH):
                        S.op("pe", lambda e, ps=ps, k=k, c0=c0, sl=sl: e.matmul(
                                ps[:], lhsT=w1c[:, k, c0:c0 + 128], rhs=self.xn[:, k, sl], start=(k == 0), stop=(k == NCH - 1)),
                             reads=[w1c[:, k, c0:c0 + 128], self.xn[:, k, sl]], writes=[ps[:]])
                sgt = sg[ix % 2]
                S.op("act", lambda e, psG=psG, sgt=sgt, j=j: e.activation(out=sgt[:], in_=psG[:], func=AF.Sigmoid, bias=pvc("cb1", NCH + j)),
                     reads=[psG[:]], writes=[sgt[:]])
                S.op("dve", lambda e, psA=psA, sgt=sgt, j=j, sl=sl: e.scalar_tensor_tensor(
                        out=ub[:, j, PAD + sl.start:PAD + sl.stop], in0=psA[:], scalar=pvc("cb1", j), in1=sgt[:], op0=ALU.add, op1=ALU.mult),
                     reads=[psA[:], sgt[:]], writes=[ub[:, j, PAD + sl.start:PAD + sl.stop]])
                ix += 1
        if stop <= 2:
            return
        for j in range(NCH):
            dgs = dg[(j % 2) * 31:(j % 2) * 31 + 31]
            for w in range(31):
                S.op("dve", lambda e, w=w, j=j, dgs=dgs: e.tensor_scalar(out=dgs[w][:], in0=self.ident_bf[:], scalar1=pvc("cdw", w * NCH + j),
                                                                         scalar2=None, op0=ALU.mult),
                     reads=[self.ident_bf[:]], writes=[dgs[w][:]])
            for tb in range(NTB):
                sl = slice(tb * TB, (tb + 1) * TB)
                ps = self.ps[(j * NTB + tb) % 8]
                for w in range(31):
                    c0 = tb * TB + (PAD - 30) + w
                    S.op("pe", lambda e, ps=ps, w=w, j=j, c0=c0, dgs=dgs: e.matmul(
                            ps[:], lhsT=dgs[w][:], rhs=ub[:, j, c0:c0 + TB], start=(w == 0), stop=(w == 30)),
                         reads=[dgs[w][:], ub[:, j, c0:c0 + TB]], writes=[ps[:]])
                if (j * NTB + tb) % 2 == 0:
                    S.op("act", lambda e, ps=ps, j=j, sl=sl: e.activation(out=cbf[:, j, sl], in_=ps[:], func=AF.Identity, bias=pvc("cbd", j)),
                         reads=[ps[:]], writes=[cbf[:, j, sl]])
                else:
                    S.op("dve", lambda e, ps=ps, j=j, sl=sl: e.tensor_scalar(out=cbf[:, j, sl], in0=ps[:], scalar1=pvc("cbd", j), scalar2=None, op0=ALU.add),
                         reads=[ps[:]], writes=[cbf[:, j, sl]])
        if stop <= 3:
            return
        z = self.xn
        csqs = [self.carve(U_OFF + i * 8192, (128, NCH, TB), BF16) for i in range(2)]
        sts = [[self.carve(U_OFF + 16384 + s_ * 8192 + i * 2048, (128, TB), F32) for i in range(4)] for s_ in range(2)]
        tt = [self.carve(DG_OFF + i * 2048, (128, TB), F32) for i in range(4)]

        def ln_stats(tb):
            sl = slice(tb * TB, (tb + 1) * TB)
            csq = csqs[tb % 2]
            mean, msq, rstd, nmr = sts[tb % 2]
            if tb % 2 == 0:
                S.op("act", lambda e: e.activation(out=csq[:], in_=cbf[:, :, sl], func=AF.Square), reads=[cbf[:, :, sl]], writes=[csq[:]])
            else:
                S.op("pool", lambda e: e.tensor_tensor(out=csq[:], in0=cbf[:, :, sl], in1=cbf[:, :, sl], op=ALU.mult),
                     reads=[cbf[:, :, sl]], writes=[csq[:]])
            psM = self.ps[(2 * tb) % 8]
            psQ = self.ps[(2 * tb + 1) % 8]
            for j in range(NCH):
                S.op("pe", lambda e, j=j: e.matmul(psM[:], lhsT=self.ones_bf[:], rhs=cbf[:, j, sl], start=(j == 0), stop=(j == NCH - 1)),
                     reads=[cbf[:, j, sl]], writes=[psM[:]])
            for j in range(NCH):
                S.op("pe", lambda e, j=j: e.matmul(psQ[:], lhsT=self.ones_bf[:], rhs=csq[:, j, :], start=(j == 0), stop=(j == NCH - 1)),
                     reads=[csq[:, j, :]], writes=[psQ[:]])
            S.op("act", lambda e: e.activation(out=mean[:], in_=psM[:], func=AF.Copy, scale=1.0 / D), reads=[psM[:]], writes=[mean[:]])
            S.op("dve", lambda e: e.tensor_tensor(out=msq[:], in0=mean[:], in1=mean[:], op=ALU.mult), reads=[mean[:]], writes=[msq[:]])
            S.op("dve", lambda e: e.scalar_tensor_tensor(out=rstd[:], in0=psQ[:], scalar=1.0 / D, in1=msq[:], op0=ALU.mult, op1=ALU.subtract),
                 reads=[psQ[:], msq[:]], writes=[rstd[:]])
            S.op("act", lambda e: e.activation(out=rstd[:], in_=rstd[:], func=AF.Ln, bias=self.eps_t[:, 0:1]), reads=[rstd[:]], writes=[rstd[:]])
            S.op("act", lambda e: e.activation(out=rstd[:], in_=rstd[:], func=AF.Exp, scale=-0.5), reads=[rstd[:]], writes=[rstd[:]])
            S.op("dve", lambda e: e.scalar_tensor_tensor(out=nmr[:], in0=mean[:], scalar=-1.0, in1=rstd[:], op0=ALU.mult, op1=ALU.mult),
                 reads=[mean[:], rstd[:]], writes=[nmr[:]])

        def ln_apply(tb):
            sl = slice(tb * TB, (tb + 1) * TB)
            mean, msq, rstd, nmr = sts[tb % 2]
            for j in range(NCH):
                t = tt[j % 4]
                S.op("dve", lambda e, t=t, j=j: e.tensor_tensor(out=t[:], in0=cbf[:, j, sl], in1=rstd[:], op=ALU.mult),
                     reads=[cbf[:, j, sl], rstd[:]], writes=[t[:]])
                S.op("dve" if j % 2 == 0 else "pool", lambda e, t=t: e.tensor_tensor(out=t[:], in0=t[:], in1=nmr[:], op=ALU.add),
                     reads=[t[:], nmr[:]], writes=[t[:]])
                S.op("act", lambda e, t=t, j=j: e.activation(out=z[:, j, sl], in_=t[:], func=AF.Silu, scale=pvc("clg", j), bias=pvc("clb", j)),
                     reads=[t[:]], writes=[z[:, j, sl]])

        def pw2(tb):
            sl = slice(tb * TB, (tb + 1) * TB)
            for m in range(NCH):
                ps = self.ps[(tb * NCH + m) % 8]
                for j in range(NCH):
                    S.op("pe", lambda e, ps=ps, m=m, j=j: e.matmul(
                            ps[:], lhsT=w2c[:, j, m * 128:(m + 1) * 128], rhs=z[:, j, sl], start=(j == 0), stop=(j == NCH - 1)),
                         reads=[w2c[:, j, m * 128:(m + 1) * 128], z[:, j, sl]], writes=[ps[:]])
                S.op("dve", lambda e, ps=ps, m=m: e.scalar_tensor_tensor(
                        out=self.h[:, m, sl], in0=ps[:], scalar=pvc("cb2", m), in1=self.h[:, m, sl], op0=ALU.add, op1=ALU.add),
                     reads=[ps[:], self.h[:, m, sl]], writes=[self.h[:, m, sl]])

        ln_stats(0)
        for tb in range(NTB):
            if tb + 1 < NTB:
                ln_stats(tb + 1)
            ln_apply(tb)
            pw2(tb)

    def hgrn(self, li):
        S = self.S
        import os
        stop = int(os.environ.get("HG_STOP", "9"))
        Z_OFF, QT_OFF, KT_OFF, V_OFF, WS_OFF, SC_OFF = 0, 32768, 49152, 65536, 81920, 98304
        z = self.carve(Z_OFF, (128, NCH, T), BF16)
        qt = self.carve(QT_OFF, (128, 4, T), BF16)
        kt = self.carve(KT_OFF, (128, 4, T), BF16)
        vtm = self.carve(V_OFF, (128, 16, 512), BF16)
        wsl = [self.carve(WS_OFF + i * 8192, (128, NCH, 512), BF16) for i in range(2)]
        wd = self.w[f"hgrn_w_qfig_{li}"].rearrange("(k p) f -> p k f", p=128)
        pvc = lambda nm, i: self.pvec[:, PV[nm][0] + i:PV[nm][0] + i + 1]

        def load(slot, grp):
            S.dma("pool", f"hw{slot}", [(wsl[slot][:], wd[:, :, grp * 512:(grp + 1) * 512])], writes=[wsl[slot][:]])

        msk = self.carve(SC_OFF, (128, TB), BF16)
        mk4 = self.carve(SC_OFF + 1024, (128, 4, 128), BF16)
        ebl = self.carve(SC_OFF + 2048, (128, 4, 32), F32)
        le = self.carve(SC_OFF + 2560, (128, 4, NCH), F32)
        se = self.carve(SC_OFF + 2688, (128, NCH), F32)
        lbv = self.carve(SC_OFF + 2720, (128, NCH), F32)
        oml = self.carve(SC_OFF + 2752, (128, NCH), F32)
        mi = self.iota_i[:, 0:128]
        mf = self.carve(SC_OFF + 3584, (128, 128), F32)

        load(0, 0)
        load(1, 2)
        self.norm_to_xn(li, scratch_off=Z_OFF)
        c0 = PV["hlb"][0]
        S.op("act", lambda e: e.activation(out=le[:].rearrange("p a b -> p (a b)"), in_=self.pvec[:, c0:c0 + 4 * NCH], func=AF.Exp),
             reads=[self.pvec[:, c0:c0 + 4 * NCH]], writes=[le[:]])
        S.op("dve", lambda e: e.tensor_tensor(out=se[:], in0=le[:, 0, :], in1=le[:, 1, :], op=ALU.add), reads=[le[:]], writes=[se[:]])
        S.op("dve", lambda e: e.tensor_tensor(out=se[:], in0=se[:], in1=le[:, 2, :], op=ALU.add), reads=[le[:], se[:]], writes=[se[:]])
        S.op("dve", lambda e: e.tensor_tensor(out=se[:], in0=se[:], in1=le[:, 3, :], op=ALU.add), reads=[le[:], se[:]], writes=[se[:]])
        S.op("dve", lambda e: e.reciprocal(out=se[:], in_=se[:]), reads=[se[:]], writes=[se[:]])
        if li == 0:
            S.op("dve", lambda e: e.memset(lbv[:], 0.0), writes=[lbv[:]])
        else:
            S.op("dve", lambda e: e.tensor_copy(out=lbv[:], in_=le[:, 1, :]), reads=[le[:]], writes=[lbv[:]])
            for i in range(2, li + 1):
                S.op("dve", lambda e, i=i: e.tensor_tensor(out=lbv[:], in0=lbv[:], in1=le[:, i, :], op=ALU.add), reads=[le[:], lbv[:]], writes=[lbv[:]])
        S.op("dve", lambda e: e.tensor_tensor(out=lbv[:], in0=lbv[:], in1=se[:], op=ALU.mult), reads=[lbv[:], se[:]], writes=[lbv[:]])
        S.op("dve", lambda e: e.tensor_scalar(out=oml[:], in0=lbv[:], scalar1=-1.0, scalar2=1.0, op0=ALU.mult, op1=ALU.add),
             reads=[lbv[:]], writes=[oml[:]])
        S.op("pool", lambda e: e.memset(msk[:], 1.0), writes=[msk[:]])
        mskv = msk[:].rearrange("p (c t) -> p c t", t=64)
        S.op("pool", lambda e: e.memset(mskv[:, :, 0:1], 0.0), reads=[msk[:]], writes=[msk[:]])
        S.op("pool", lambda e: e.iota(mi[:], pattern=[[1, 128]], base=0, channel_multiplier=-1), writes=[mi[:]])
        S.op("dve", lambda e: e.tensor_copy(out=mf[:], in_=mi[:]), reads=[mi[:]], writes=[mf[:]])
        S.op("dve", lambda e: e.tensor_single_scalar(out=mf[:], in_=mf[:], scalar=0.0, op=ALU.is_ge), reads=[mf[:]], writes=[mf[:]])
        S.op("dve", lambda e: e.memset(mf[0:64, 64:128], 0.0), reads=[mf[:]], writes=[mf[:]])
        for hd in range(4):
            S.op("dve", lambda e, hd=hd: e.tensor_copy(out=mk4[:, hd, :], in_=mf[:]), reads=[mf[:]], writes=[mk4[:, hd, :]])
        if stop <= 1:
            return

        T1 = self.carve(Z_OFF + 16384, (128, T), F32)
        T2 = self.carve(Z_OFF + 24576, (128, T), F32)
        T3 = self.carve(V_OFF, (128, T), F32)
        T4 = [self.carve(V_OFF + 8192 + i * 4096, (128, T), BF16) for i in range(2)]
        QS = 128 ** -0.5

        for hf in range(2):
            if hf == 1:
                load(0, 1)
                load(1, 3)
            for hd in range(4):
                gh = hf * 4 + hd
                t4 = T4[hd % 2]
                for slot in (1, 0):
                    for tb in range(NTB):
                        sl = slice(tb * TB, (tb + 1) * TB)
                        ps = self.ps[2 * tb] if slot == 1 else self.ps[2 * tb + 1]
                        for k in range(NCH):
                            S.op("pe", lambda e, ps=ps, slot=slot, k=k, hd=hd, sl=sl: e.matmul(
                                    ps[:], lhsT=wsl[slot][:, k, hd * 128:(hd + 1) * 128], rhs=self.xn[:, k, sl],
                                    start=(k == 0), stop=(k == NCH - 1)),
                                 reads=[wsl[slot][:, k, hd * 128:(hd + 1) * 128], self.xn[:, k, sl]], writes=[ps[:]])
                    if hd == 3 and slot == 1:
                        load(1, 4 + hf)
                for tb in range(NTB):
                    sl = slice(tb * TB, (tb + 1) * TB)
                    psF = self.ps[2 * tb]
                    S.op("act", lambda e, psF=psF, sl=sl: e.activation(out=T1[:, sl], in_=psF[:], func=AF.Sigmoid), reads=[psF[:]], writes=[T1[:, sl]])
                for tb in range(NTB):
                    sl = slice(tb * TB, (tb + 1) * TB)
                    psQ = self.ps[2 * tb + 1]
                    S.op("act", lambda e, psQ=psQ, sl=sl, t4=t4: e.activation(out=t4[:, sl], in_=psQ[:], func=AF.Silu), reads=[psQ[:]], writes=[t4[:, sl]])
                S.op("dve", lambda e, gh=gh: e.tensor_scalar(out=T1[:], in0=T1[:], scalar1=oml[:, gh:gh + 1], scalar2=lbv[:, gh:gh + 1],
                                                             op0=ALU.mult, op1=ALU.add),
                     reads=[T1[:], oml[:], lbv[:]], writes=[T1[:]])
                S.op("act", lambda e: e.activation(out=T2[:], in_=T1[:], func=AF.Ln), reads=[T1[:]], writes=[T2[:]])
                for tb in range(NTB):
                    sl = slice(tb * TB, (tb + 1) * TB)
                    S.op("dve", lambda e, sl=sl: e.tensor_tensor_scan(out=T3[:, sl], data0=msk[:], data1=T2[:, sl], initial=0.0, op0=ALU.mult, op1=ALU.add),
                         reads=[msk[:], T2[:, sl]], writes=[T3[:, sl]])
                S.op("act", lambda e: e.activation(out=T2[:], in_=T3[:], func=AF.Exp, scale=-1.0), reads=[T3[:]], writes=[T2[:]])
                S.op("pool", lambda e: e.tensor_scalar(out=T1[:], in0=T1[:], scalar1=-1.0, scalar2=1.0, op0=ALU.mult, op1=ALU.add),
                     reads=[T1[:]], writes=[T1[:]])
                S.op("dve", lambda e, hd=hd: e.tensor_tensor(out=kt[:, hd, :], in0=T1[:], in1=T2[:], op=ALU.mult),
                     reads=[T1[:], T2[:]], writes=[kt[:, hd, :]])
                S.op("act", lambda e: e.activation(out=T2[:], in_=T3[:], func=AF.Exp), reads=[T3[:]], writes=[T2[:]])
                ebv = T2[:].rearrange("p (c t) -> p c t", t=64)
                S.op("pool", lambda e, hd=hd, ebv=ebv: e.tensor_copy(out=ebl[:, hd, :], in_=ebv[:, :, 63]), reads=[T2[:]], writes=[ebl[:, hd, :]])
                S.op("dve", lambda e, hd=hd, t4=t4: e.scalar_tensor_tensor(out=qt[:, hd, :], in0=t4[:], scalar=QS, in1=T2[:], op0=ALU.mult, op1=ALU.mult),
                     reads=[t4[:], T2[:]], writes=[qt[:, hd, :]])
                if os.environ.get("HG_DBG") == "1":
                    ov = self.outT.rearrange("(c p) t -> p c t", p=128)
                    S.dma("sp", "dbg", [(ov[:, 0, :], T1[:]), (ov[:, 1, :], T2[:]), (ov[:, 2, :], T3[:])], reads=[T1[:], T2[:], T3[:]])
                    S.wait_all_dma("sp", ["dbg"])
                    self.dbg_stop = True
                    return
            for i in range(16):
                ps = self.ps[i % 8]
                for k in range(NCH):
                    S.op("pe", lambda e, ps=ps, k=k, i=i: e.matmul(ps[:], lhsT=self.xn[:, k, i * 128:(i + 1) * 128], rhs=wsl[1][:, k, :],
                                                                   start=(k == 0), stop=(k == NCH - 1)),
                         reads=[self.xn[:, k, i * 128:(i + 1) * 128], wsl[1][:, k, :]], writes=[ps[:]])
                S.op("dve", lambda e, ps=ps, i=i: e.tensor_copy(out=vtm[:, i, :], in_=ps[:]), reads=[ps[:]], writes=[vtm[:, i, :]])
            if os.environ.get("HG_DBG") == "2":
                ov = self.outT.rearrange("(c p) t -> p c t", p=128)
                S.dma("pool", "dbg", [(ov[:, 0:4, :], kt[:]), (ov[:, 4:8, :], qt[:])], reads=[kt[:], qt[:]])
                S.wait_all_dma("pool", ["dbg"])
                self.dbg_stop = True
                return
            if stop <= 2:
                continue
            def _recur(hf, R):
                At = [self.carve(R + i * 1024, (128, 4, 128), BF16) for i in range(2)]
                ktm = [self.carve(R + 2048 + i * 1024, (128, 4, 128), BF16) for i in range(2)]
                Sbf = [self.carve(R + 4096 + i * 1024, (128, 4, 128), BF16) for i in range(2)]
                Sf = self.carve(R + 6144, (128, 4, 128), F32)
                tmpS = [self.carve(R + 8192 + i * 2048, (128, 4, 128), F32) for i in range(2)]
                osq = self.carve(R + 12288, (128, 512), BF16)
                rstd = self.carve(R + 13312, (128, 4, 128), F32)

                Sbf = [self.carve(R + 4096 + i * 1024, (128, 4, 128), BF16) for i in range(2)] + \
                      [self.carve(R + 15360, (128, 4, 128), BF16), self.carve(SC_OFF + 4096, (128, 4, 128), BF16)]
                flat = lambda t: t[:].rearrange("p a b -> p (a b)")

                def pre(i):
                    tl = slice(i * 128, (i + 1) * 128)
                    psA = self.ps[0]
                    psK = self.ps[1]
                    for hd in range(4):
                        cs = slice(hd * 128, (hd + 1) * 128)
                        S.op("pe", lambda e, psA=psA, hd=hd, cs=cs, tl=tl: e.matmul(psA[:, cs], lhsT=kt[:, hd, tl], rhs=qt[:, hd, tl], start=True, stop=True),
                             reads=[kt[:, hd, tl], qt[:, hd, tl]], writes=[psA[:, cs]])
                    S.op("dve", lambda e, psA=psA, i=i: e.tensor_tensor(out=flat(At[i % 2]), in0=psA[:], in1=flat(mk4), op=ALU.mult),
                         reads=[psA[:], mk4[:]], writes=[At[i % 2][:]])
                    for hd in range(4):
                        cs = slice(hd * 128, (hd + 1) * 128)
                        S.op("pe", lambda e, psK=psK, hd=hd, cs=cs, tl=tl: e.matmul(psK[:, cs], lhsT=kt[:, hd, tl], rhs=self.ident_bf[:], start=True, stop=True),
                             reads=[kt[:, hd, tl], self.ident_bf[:]], writes=[psK[:, cs]])
                    S.op("dve", lambda e, psK=psK, i=i: e.tensor_copy(out=flat(ktm[i % 2]), in_=psK[:]),
                         reads=[psK[:]], writes=[ktm[i % 2][:]])

                def stateP(i, cc):
                    if 2 * i + cc == 31:
                        return
                    psS = self.ps[4 + cc]
                    pr = slice(cc * 64, (cc + 1) * 64)
                    for hd in range(4):
                        cs = slice(hd * 128, (hd + 1) * 128)
                        S.op("pe", lambda e, psS=psS, hd=hd, cs=cs, pr=pr, i=i: e.matmul(psS[:, cs], lhsT=ktm[i % 2][pr, hd, :], rhs=vtm[pr, i, cs],
                                                                                       start=True, stop=True),
                             reads=[ktm[i % 2][pr, hd, :], vtm[pr, i, cs]], writes=[psS[:, cs]])

                U = [self.carve(R + 6144, (128, 4, 128), F32), self.carve(R + 8192, (128, 4, 128), F32)]
                S.op("dve", lambda e: e.memset(U[1][:], 0.0), writes=[U[1][:]])

                def chain(i, cc):
                    c = 2 * i + cc
                    if c == 31:
                        return
                    psS = self.ps[4 + cc]
                    cp = max(c - 1, 0)
                    for hd in range(4):
                        cs = slice(hd * 128, (hd + 1) * 128)
                        S.op("dve", lambda e, hd=hd, cs=cs, c=c, cp=cp, psS=psS: e.scalar_tensor_tensor(
                                out=U[c % 2][:, hd, :], in0=U[(c + 1) % 2][:, hd, :], scalar=ebl[:, hd, cp:cp + 1], in1=psS[:, cs],
                                op0=ALU.mult, op1=ALU.add),
                             reads=[U[(c + 1) % 2][:, hd, :], ebl[:, hd, cp:cp + 1], psS[:, cs]], writes=[U[c % 2][:, hd, :]])
                    for hd in range(4):
                        if hd < 2:
                            S.op("act", lambda e, hd=hd, c=c: e.activation(out=Sbf[c % 4][:, hd, :], in_=U[c % 2][:, hd, :], func=AF.Copy,
                                                                           scale=ebl[:, hd, c:c + 1]),
                                 reads=[U[c % 2][:, hd, :], ebl[:, hd, c:c + 1]], writes=[Sbf[c % 4][:, hd, :]])
                        else:
                            S.op("pool", lambda e, hd=hd, c=c: e.tensor_scalar(out=Sbf[c % 4][:, hd, :], in0=U[c % 2][:, hd, :],
                                                                               scalar1=ebl[:, hd, c:c + 1], scalar2=1.0, op0=ALU.mult, op1=ALU.mult),
                                 reads=[U[c % 2][:, hd, :], ebl[:, hd, c:c + 1]], writes=[Sbf[c % 4][:, hd, :]])

                def omm(i):
                    psO = self.ps[2 + i % 2]
                    for hd in range(4):
                        cs = slice(hd * 128, (hd + 1) * 128)
                        mms = [(psO[:, cs], vtm[:, i, cs], At[i % 2][:, hd, :])]
                        if i > 0:
                            mms.append((psO[:, hd * 128:hd * 128 + 64], Sbf[(2 * i - 1) % 4][:, hd, :], qt[:, hd, i * 128:i * 128 + 64]))
                        mms.append((psO[:, hd * 128 + 64:hd * 128 + 128], Sbf[(2 * i) % 4][:, hd, :], qt[:, hd, i * 128 + 64:i * 128 + 128]))
                        for n_, (o_, l_, r_) in enumerate(mms):
                            S.op("pe", lambda e, o_=o_, l_=l_, r_=r_, n_=n_, nm=len(mms): e.matmul(o_, lhsT=l_, rhs=r_, start=(n_ == 0), stop=(n_ == nm - 1)),
                                 reads=[l_, r_], writes=[o_])

                def post_a(i):
                    psO = self.ps[2 + i % 2]
                    S.op("act", lambda e: e.activation(out=osq[:], in_=psO[:], func=AF.Square), reads=[psO[:]], writes=[osq[:]])

                def post_b(i):
                    psR = self.ps[6 + i % 2]
                    S.op("pe", lambda e: e.matmul(psR[:], lhsT=self.ones_bf[:], rhs=osq[:], start=True, stop=True), reads=[osq[:]], writes=[psR[:]])

                def post_c(i):
                    psR = self.ps[6 + i % 2]
                    rflat = flat(rstd)
                    S.op("act", lambda e: e.activation(out=rflat, in_=psR[:], func=AF.Ln, scale=1.0 / 128, bias=self.eps_t[:, 0:1]),
                         reads=[psR[:]], writes=[rstd[:]])
                    S.op("act", lambda e: e.activation(out=rflat, in_=rflat, func=AF.Exp, scale=-0.5), reads=[rstd[:]], writes=[rstd[:]])

                def post_d(i, hf=hf):
                    tl = slice(i * 128, (i + 1) * 128)
                    psO = self.ps[2 + i % 2]
                    S.op("dve", lambda e: e.tensor_tensor(out=z[:, hf * 4:(hf + 1) * 4, tl], in0=psO[:].rearrange("p (a b) -> p a b", a=4),
                                                          in1=rstd[:], op=ALU.mult),
                         reads=[psO[:], rstd[:]], writes=[z[:, hf * 4:(hf + 1) * 4, tl]])

                pre(0)
                for i in range(17):
                    if i + 1 < 16:
                        pre(i + 1)
                    if i > 0:
                        post_a(i - 1)
                    if i < 16:
                        stateP(i, 0)
                        stateP(i, 1)
                    if i > 0:
                        post_b(i - 1)
                    if i < 16:
                        chain(i, 0)
                    if i > 0:
                        post_c(i - 1)
                    if i < 16:
                        chain(i, 1)
                        omm(i)
                    if i > 0:
                        post_d(i - 1)
                if os.environ.get("HG_DBG") == "7" and hf == 1:
                    ov = self.outT.rearrange("(c p) t -> p c t", p=128)
                    S.dma("pool", "dbg", [(ov[:, :, :], z[:, :, :])], reads=[z[:]])
                    S.wait_all_dma("pool", ["dbg"])
                    self.dbg_stop = True
                    return
                if os.environ.get("HG_DBG") == "4" and hf == 1:
                    ov = self.outT.rearrange("(c p) t -> p c t", p=128)
                    S.dma("pool", "dbg", [(ov[:, 0:4, :], z[:, 4:8, :])], reads=[z[:]])
                    S.wait_all_dma("pool", ["dbg"])
                    self.dbg_stop = True
                    return
                if os.environ.get("HG_DBG") == "3":
                    ov = self.outT.rearrange("(c p) t -> p c t", p=128)
                    S.dma("pool", "dbg", [(ov[:, 0:4, :], z[:, 0:4, :])], reads=[z[:]])
                    S.wait_all_dma("pool", ["dbg"])
                    self.dbg_stop = True
                    return

            _recur(hf, (Z_OFF + 16384) if hf == 0 else WS_OFF)
        if stop <= 3:
            return
        load(0, 6)
        load(1, 7)
        wo = self.carve(V_OFF, (128, NCH, D), BF16)
        S.dma("pool", "hwo", [(wo[:], self.w[f"hgrn_w_o_{li}"].rearrange("(k p) f -> p k f", p=128))], writes=[wo[:]])
        sgb = [self.carve(QT_OFF + i * 1024, (128, TB), BF16) for i in range(4)]
        ix = 0
        for gh in range(NCH):
            for tb in range(NTB):
                sl = slice(tb * TB, (tb + 1) * TB)
                ps = self.ps[ix % 8]
                for k in range(NCH):
                    S.op("pe", lambda e, ps=ps, k=k, gh=gh, sl=sl: e.matmul(
                            ps[:], lhsT=wsl[gh // 4][:, k, (gh % 4) * 128:(gh % 4 + 1) * 128], rhs=self.xn[:, k, sl],
                            start=(k == 0), stop=(k == NCH - 1)),
                         reads=[wsl[gh // 4][:, k, (gh % 4) * 128:(gh % 4 + 1) * 128], self.xn[:, k, sl]], writes=[ps[:]])
                sg = sgb[ix % 4]
                S.op("act", lambda e, ps=ps, sg=sg: e.activation(out=sg[:], in_=ps[:], func=AF.Silu), reads=[ps[:]], writes=[sg[:]])
                if os.environ.get("HG_DBG") == "6":
                    S.op("dve", lambda e, sg=sg, gh=gh, sl=sl: e.tensor_copy(out=z[:, gh, sl], in_=sg[:]), reads=[sg[:]], writes=[z[:, gh, sl]])
                    ix += 1
                    continue
                if os.environ.get("HG_Z3", "1") == "1":
                    S.op("dve", lambda e, sg=sg, gh=gh, sl=sl: e.scalar_tensor_tensor(out=z[:, gh, sl], in0=sg[:], scalar=pvc("hng", gh), in1=z[:, gh, sl],
                                                                                       op0=ALU.mult, op1=ALU.mult),
                         reads=[z[:, gh, sl], sg[:]], writes=[z[:, gh, sl]])
                else:
                    S.op("dve", lambda e, sg=sg, gh=gh, sl=sl: e.tensor_tensor(out=z[:, gh, sl], in0=z[:, gh, sl], in1=sg[:], op=ALU.mult),
                         reads=[z[:, gh, sl], sg[:]], writes=[z[:, gh, sl]])
                ix += 1
        if os.environ.get("HG_DBG") in ("5", "6"):
            ov = self.outT.rearrange("(c p) t -> p c t", p=128)
            S.dma("pool", "dbg", [(ov[:, :, :], z[:, :, :])], reads=[z[:]])
            S.wait_all_dma("pool", ["dbg"])
            self.dbg_stop = True
            return
        for tb in range(NTB):
            for m in range(NCH):
                sl = slice(tb * TB, (tb + 1) * TB)
                ps = self.ps[(tb * NCH + m) % 8]
                for j in range(NCH):
                    S.op("pe", lambda e, ps=ps, m=m, j=j, sl=sl: e.matmul(
                            ps[:], lhsT=wo[:, j, m * 128:(m + 1) * 128], rhs=z[:, j, sl], start=(j == 0), stop=(j == NCH - 1)),
                         reads=[wo[:, j, m * 128:(m + 1) * 128], z[:, j, sl]], writes=[ps[:]])
                S.op("dve", lambda e, ps=ps, m=m, sl=sl: e.tensor_tensor(out=self.h[:, m, sl], in0=self.h[:, m, sl], in1=ps[:], op=ALU.add),
                     reads=[self.h[:, m, sl], ps[:]], writes=[self.h[:, m, sl]])

    def mlp(self, li):
        S = self.S
        G = 8
        NG = DFF // (G * 128)
        HID_OFF = 0
        W_OFF = 32768
        hid = self.carve(HID_OFF, (128, G, T), BF16)
        w1s = [self.carve(W_OFF + s * 32768, (128, NCH, G * 128), BF16) for s in range(2)]
        w2s = [self.carve(W_OFF + s * 32768 + 16384, (128, G, D), BF16) for s in range(2)]
        w1 = self.w[f"mlp_w1_{li}"].rearrange("(k p) f -> p k f", p=128)
        w2 = self.w[f"mlp_w2_{li}"].rearrange("(c p) m -> p c m", p=128)

        def load(g):
            s = g % 2
            S.dma("pool", f"mw1_{s}", [(w1s[s][:], w1[:, :, g * G * 128:(g + 1) * G * 128])], writes=[w1s[s][:]])
            S.dma("pool", f"mw2_{s}", [(w2s[s][:], w2[:, g * G:(g + 1) * G, :])], writes=[w2s[s][:]])

        load(0)
        self.norm_to_xn(4 + li, scratch_off=HID_OFF)
        for g in range(NG):
            if g + 1 < NG:
                load(g + 1)
            s = g % 2
            for c in range(G):
                banks = [self.ps[(c % 2) * 4 + tb] for tb in range(NTB)]
                for k in range(NCH):
                    for tb in range(NTB):
                        sl = slice(tb * TB, (tb + 1) * TB)
                        ps = banks[tb]
                        S.op("pe", lambda e, ps=ps, s=s, k=k, c=c, sl=sl: e.matmul(
                                ps[:], lhsT=w1s[s][:, k, c * 128:(c + 1) * 128], rhs=self.xn[:, k, sl],
                                start=(k == 0), stop=(k == NCH - 1)),
                             reads=[w1s[s][:, k, c * 128:(c + 1) * 128], self.xn[:, k, sl]], writes=[ps[:]])
                for tb in range(NTB):
                    sl = slice(tb * TB, (tb + 1) * TB)
                    ps = banks[tb]
                    S.op("act", lambda e, ps=ps, c=c, sl=sl: e.activation(out=hid[:, c, sl], in_=ps[:], func=AF.Relu),
                         reads=[ps[:]], writes=[hid[:, c, sl]])
                    S.op("pool", lambda e, c=c, sl=sl: e.tensor_tensor(out=hid[:, c, sl], in0=hid[:, c, sl], in1=hid[:, c, sl], op=ALU.mult),
                         reads=[hid[:, c, sl]], writes=[hid[:, c, sl]])
            if g < NG - 1:
                for m in range(NCH):
                    banks = [self.ps[(m % 2) * 4 + tb] for tb in range(NTB)]
                    for c in range(G):
                        for tb in range(NTB):
                            sl = slice(tb * TB, (tb + 1) * TB)
                            ps = banks[tb]
                            S.op("pe", lambda e, ps=ps, s=s, m=m, c=c, sl=sl: e.matmul(
                                    ps[:], lhsT=w2s[s][:, c, m * 128:(m + 1) * 128], rhs=hid[:, c, sl],
                                    start=(c == 0), stop=(c == G - 1)),
                                 reads=[w2s[s][:, c, m * 128:(m + 1) * 128], hid[:, c, sl]], writes=[ps[:]])
                    for tb in range(NTB):
                        sl = slice(tb * TB, (tb + 1) * TB)
                        ps = banks[tb]
                        S.op("dve", lambda e, ps=ps, m=m, sl=sl: e.tensor_tensor(out=self.h[:, m, sl], in0=self.h[:, m, sl], in1=ps[:], op=ALU.add),
                             reads=[self.h[:, m, sl], ps[:]], writes=[self.h[:, m, sl]])
            else:
                for tb in range(NTB):
                    for m in range(NCH):
                        sl = slice(tb * TB, (tb + 1) * TB)
                        ps = self.psbank()
                        for c in range(G):
                            S.op("pe", lambda e, ps=ps, s=s, m=m, c=c, sl=sl: e.matmul(
                                    ps[:], lhsT=w2s[s][:, c, m * 128:(m + 1) * 128], rhs=hid[:, c, sl],
                                    start=(c == 0), stop=(c == G - 1)),
                                 reads=[w2s[s][:, c, m * 128:(m + 1) * 128], hid[:, c, sl]], writes=[ps[:]])
                        S.op("dve", lambda e, ps=ps, m=m, sl=sl: e.tensor_tensor(out=self.h[:, m, sl], in0=self.h[:, m, sl], in1=ps[:], op=ALU.add),
                             reads=[self.h[:, m, sl], ps[:]], writes=[self.h[:, m, sl]])

    def build(self):
        mixers, mlps = self.do_mixers, self.do_mlps
        self.declare()
        self.eps_t = self.sb("eps_t", (128, 1), F32)
        self.es_t = self.sb("es_t", (128, 2 * NCH), F32)
        self.ident_bf = self.sb("ident_bf", (128, 128), BF16)
        self.iota_i = self.sb("iota_i", (128, 256), mybir.dt.int32)
        idi = self.iota_i[:, 0:128]
        self.S.op("pool", lambda e: e.iota(idi[:], pattern=[[1, 128]], base=0, channel_multiplier=-1), writes=[idi[:]])
        self.S.op("dve", lambda e: e.tensor_single_scalar(out=self.ident_bf[:], in_=idi[:], scalar=0, op=ALU.is_equal),
                  reads=[idi[:]], writes=[self.ident_bf[:]])
        self.S.op("dve", lambda e: e.memset(self.eps_t[:], EPS), writes=[self.eps_t[:]])
        self.setup()
        for li in self.layers:
            if mixers:
                [self.attention, self.conv, self.hgrn][LAYER_KIND[li]](li)
            if mlps:
                self.mlp(li)
        if getattr(self, "dbg_stop", False):
            pass
        elif self.final_norm:
            self.final()
        else:
            self.store_h()
        self.S.emit(self.nc, self.es)
        self.es.close()
        return self.nc


def _pc(v):
    v = np.asarray(v, dtype=np.float32).reshape(-1, NCH, 128)
    return np.ascontiguousarray(v.transpose(2, 0, 1).reshape(128, -1))


def _in_maps(inputs, layers=(0, 1, 2, 3), mixers=True, mlps=True):
    x = np.asarray(inputs["x"], dtype=np.float32)
    shared = {}
    for li in layers:
        j = li // 3
        if mlps:
            shared[f"mlp_w1_{li}"] = np.ascontiguousarray(np.asarray(inputs["mlp_w1"][li], dtype=np.float32))
            shared[f"mlp_w2_{li}"] = np.ascontiguousarray(np.asarray(inputs["mlp_w2"][li], dtype=np.float32))
        if mixers:
            for nm, shp in MIXER_W[LAYER_KIND[li]]:
                shared[f"{nm}_{li}"] = np.ascontiguousarray(np.asarray(inputs[nm][j], dtype=np.float32))
    pv = np.zeros((128, PV_N), np.float32)
    gains = np.concatenate([np.asarray(inputs["norm_mixer"], np.float32), np.asarray(inputs["norm_mlp"], np.float32),
                            np.asarray(inputs["final_norm"], np.float32).reshape(1, D)], axis=0)
    pv[:, PV["gains"][0]:PV["gains"][1]] = _pc(gains)
    sk = np.asarray(inputs["attn_sinks"], np.float32).reshape(2, NCH, 2)
    pv[:, PV["sinks"][0]:PV["sinks"][1]] = np.repeat(sk.transpose(2, 0, 1).reshape(2, 2 * NCH), 64, axis=0)
    for nm, key in (("cb1", "conv_b_pw1"), ("cdw", "conv_w_dw"), ("cbd", "conv_b_dw"), ("clg", "conv_ln_g"),
                    ("clb", "conv_ln_b"), ("cb2", "conv_b_pw2"), ("hlb", "hgrn_lower_bounds"), ("hng", "hgrn_norm_g")):
        pv[:, PV[nm][0]:PV[nm][1]] = _pc(inputs[key])
    shared["pvec"] = pv
    maps = []
    for b in range(N_CORES):
        m = dict(shared)
        m["xT"] = np.ascontiguousarray(x[b].T)
        maps.append(m)
    return maps


def run(inputs, **bkw):
    b = Builder(**bkw)
    nc = b.build()
    maps = _in_maps(inputs, b.layers, b.do_mixers, b.do_mlps)
    res = run_bass_kernel_spmd(nc, maps, core_ids=list(range(N_CORES)))
    out = np.stack([np.asarray(r["outT"]).T for r in res.results], axis=0)
    return np.ascontiguousarray(out.astype(np.float32))


def kernel(**inputs):
    return run(inputs)
```

```python
import math
from contextlib import ExitStack

import numpy as np
import concourse.bass as bass
import concourse.mybir as mybir
from concourse.bass_utils import run_bass_kernel_spmd

F32 = mybir.dt.float32
BF16 = mybir.dt.bfloat16
AF = mybir.ActivationFunctionType
ALU = mybir.AluOpType

D = 1024
T = 2048
NCH = 8
TB = 512
NTB = T // TB
DFF = 4096
DEPTH = 4
EPS = 1e-6
N_CORES = 8

PV = {}
_pvn = 0
for _name, _n in (("gains", 9 * NCH), ("sinks", 2 * NCH), ("cb1", 16), ("cdw", 31 * NCH), ("cbd", NCH),
                  ("clg", NCH), ("clb", NCH), ("cb2", NCH), ("hlb", 4 * NCH), ("hng", NCH)):
    PV[_name] = (_pvn, _pvn + _n)
    _pvn += _n
PV_N = _pvn

_ESZ = {F32: 4, BF16: 2}


def _esz(dt):
    return _ESZ.get(dt, 4)


def _region(ap):
    esz = _esz(ap.dtype)
    dims = [tuple(d) for d in ap.ap]
    rowstride, npart = dims[0]
    if rowstride == 0:
        rowstride = 1 << 40
    p_lo = ap.offset // rowstride
    off = ap.offset % rowstride
    ivs = [(0, 1)]
    for step, cnt in reversed(dims[1:]):
        if cnt <= 1 or step == 0:
            continue
        if len(ivs) == 1 and step == ivs[0][1] - ivs[0][0]:
            ivs = [(ivs[0][0], ivs[0][0] + step * cnt)]
        elif len(ivs) * cnt <= 64 and step > 0:
            ivs = [(lo + i * step, hi + i * step) for i in range(cnt) for (lo, hi) in ivs]
        else:
            lo = min(a for a, _ in ivs)
            hi = max(b for _, b in ivs)
            if step > 0:
                ivs = [(lo, hi + step * (cnt - 1))]
            else:
                ivs = [(lo + step * (cnt - 1), hi)]
    name = ap.tensor.name
    if name.startswith("ps"):
        return (name, (p_lo // 32) * 32, -(-(p_lo + npart) // 32) * 32, [(0, 2048)])
    return (name, p_lo, p_lo + npart, [((off + a) * esz, (off + b) * esz) for a, b in ivs])


class Sched:
    COMPUTE = ("pe", "act", "dve", "pool")
    SIG_WRAP = 12000

    def __init__(self):
        self.ops = {e: [] for e in ("pe", "act", "dve", "pool", "sp")}
        self.recs = {}
        self.seen = {e: {} for e in self.ops}
        self.dma_count = {}
        self.dma_sems = []

    @staticmethod
    def _overlap(r, p_lo, p_hi, ivs):
        if r["p_hi"] <= p_lo or p_hi <= r["p_lo"]:
            return 0
        full = True
        anyov = False
        for (a, b) in r["ivs"]:
            cov = False
            for (lo, hi) in ivs:
                if a < hi and lo < b:
                    anyov = True
                    if lo <= a and b <= hi:
                        cov = True
            if not cov:
                full = False
        if not anyov:
            return 0
        if full and p_lo <= r["p_lo"] and r["p_hi"] <= p_hi:
            return 2
        return 1

    @staticmethod
    def _isub(A, B):
        out = []
        for (a, b) in A:
            cur = [(a, b)]
            for (lo, hi) in B:
                nxt = []
                for (x, y) in cur:
                    if hi <= x or y <= lo:
                        nxt.append((x, y))
                    else:
                        if x < lo:
                            nxt.append((x, lo))
                        if hi < y:
                            nxt.append((hi, y))
                cur = nxt
            out.extend(cur)
        return out

    @staticmethod
    def _iand(A, B):
        out = []
        for (a, b) in A:
            for (lo, hi) in B:
                x, y = max(a, lo), min(b, hi)
                if x < y:
                    out.append((x, y))
        return out

    def _split(self, r, p_lo, p_hi, ivs):
        outs = []
        if r["p_lo"] < p_lo:
            outs.append({"p_lo": r["p_lo"], "p_hi": p_lo, "ivs": r["ivs"], "w": list(r["w"]), "r": list(r["r"])})
        if p_hi < r["p_hi"]:
            outs.append({"p_lo": p_hi, "p_hi": r["p_hi"], "ivs": r["ivs"], "w": list(r["w"]), "r": list(r["r"])})
        q_lo, q_hi = max(p_lo, r["p_lo"]), min(p_hi, r["p_hi"])
        rest = self._isub(r["ivs"], ivs)
        if rest:
            outs.append({"p_lo": q_lo, "p_hi": q_hi, "ivs": rest, "w": list(r["w"]), "r": list(r["r"])})
        inside = {"p_lo": q_lo, "p_hi": q_hi, "ivs": self._iand(r["ivs"], ivs), "w": list(r["w"]), "r": list(r["r"])}
        return inside, outs

    def _deps(self, reads, writes, tok):
        deps = []
        rregs = [_region(a) for a in reads]
        wregs = [_region(a) for a in writes]
        for (sp, p_lo, p_hi, ivs) in rregs:
            lst = self.recs.get(sp, [])
            keep = []
            for r in lst:
                ov = self._overlap(r, p_lo, p_hi, ivs)
                if not ov:
                    keep.append(r)
                    continue
                deps.extend(r["w"])
                if sp.startswith("ps"):
                    deps.extend(t for t in r["r"] if t[0] == "c" and t[1] != tok[1])
                if ov == 2:
                    inside, outs = r, []
                else:
                    inside, outs = self._split(r, p_lo, p_hi, ivs)
                if tok[0] == "c":
                    inside["r"] = [t for t in inside["r"] if not (t[0] == "c" and t[1] == tok[1])]
                inside["r"].append(tok)
                keep.append(inside)
                keep.extend(outs)
            self.recs[sp] = keep
        for (sp, p_lo, p_hi, ivs) in wregs:
            lst = self.recs.setdefault(sp, [])
            keep = []
            for r in lst:
                ov = self._overlap(r, p_lo, p_hi, ivs)
                if not ov:
                    keep.append(r)
                    continue
                deps.extend(t for t in r["w"] if t != tok)
                deps.extend(t for t in r["r"] if t != tok)
                if ov == 1:
                    _, outs = self._split(r, p_lo, p_hi, ivs)
                    keep.extend(outs)
            keep.append({"p_lo": p_lo, "p_hi": p_hi, "ivs": ivs, "w": [tok], "r": []})
            self.recs[sp] = keep
        return deps

    def _plan_waits(self, eng, deps, extra=()):
        need = {}
        for t in list(deps) + list(extra):
            if t is None:
                continue
            key = (t[0], t[1])
            if t[0] == "c" and t[1] == eng and eng == "pe":
                continue
            need[key] = max(need.get(key, -1), t[2])
        waits = []
        seen = self.seen[eng]
        for key, v in need.items():
            if seen.get(key, -1) >= v:
                continue
            seen[key] = v
            waits.append((key[0], key[1], v))
            if key[0] == "c":
                self.ops[key[1]][v]["sig"] = True
        return waits

    def op(self, eng, fn, reads=(), writes=(), extra_waits=()):
        idx = len(self.ops[eng])
        tok = ("c", eng, idx)
        deps = self._deps(reads, writes, tok)
        waits = self._plan_waits(eng, deps, extra_waits)
        self.ops[eng].append({"fn": fn, "waits": waits, "sig": False, "dma": None})
        return tok

    def dma(self, queue, semkey, pairs, reads=(), writes=(), extra_waits=()):
        if semkey not in self.dma_count:
            self.dma_count[semkey] = 0
            self.dma_sems.append(semkey)
        self.dma_count[semkey] += 16 * len(pairs)
        tok = ("d", semkey, self.dma_count[semkey])
        deps = self._deps(reads, writes, tok)
        waits = self._plan_waits(queue, deps, extra_waits)
        for i, (o, s) in enumerate(pairs):
            self.ops[queue].append({"fn": None, "waits": waits if i == 0 else [], "sig": False,
                                    "dma": (o, s, semkey)})
        return tok

    def wait_all_dma(self, eng, semkeys):
        toks = [("d", k, self.dma_count[k]) for k in semkeys]
        waits = self._plan_waits(eng, toks)
        self.ops[eng].append({"fn": None, "waits": waits, "sig": False, "dma": None})

    def emit(self, nc, es):
        nsig = {e: sum(1 for o in self.ops[e] if o["sig"]) for e in self.COMPUTE}
        esems = {}
        for e in self.COMPUTE:
            n = max(1, -(-nsig[e] // self.SIG_WRAP))
            esems[e] = [es.enter_context(nc.semaphore(f"s_{e}{i}")) for i in range(n)]
        dsems = {k: es.enter_context(nc.semaphore(f"d_{k}")) for k in self.dma_sems}
        rank = {}
        for e in self.COMPUTE:
            r = 0
            for i, o in enumerate(self.ops[e]):
                if o["sig"]:
                    rank[(e, i)] = r
                    r += 1
        W = self.SIG_WRAP

        def replay(ename, eng):
            for i, o in enumerate(self.ops[ename]):
                for (kind, key, v) in o["waits"]:
                    if kind == "c":
                        r = rank[(key, v)]
                        eng.wait_ge(esems[key][r // W], r % W + 1)
                    else:
                        eng.wait_ge(dsems[key], v)
                if o["dma"] is not None:
                    out, src, semkey = o["dma"]
                    eng.dma_start(out=out, in_=src).then_inc(dsems[semkey], 16)
                elif o["fn"] is not None:
                    ins = o["fn"](eng)
                    if o["sig"]:
                        r = rank[(ename, i)]
                        ins.then_inc(esems[ename][r // W], 1)

        block = es.enter_context(nc.Block())

        @block.tensor
        def _(e):
            replay("pe", e)

        @block.scalar
        def _(e):
            replay("act", e)

        @block.vector
        def _(e):
            replay("dve", e)

        @block.gpsimd
        def _(e):
            replay("pool", e)

        @block.sync
        def _(e):
            replay("sp", e)


LAYER_KIND = [0, 1, 2, 0]
LAYER_J = [0, 0, 0, 1]
MIXER_W = {
    0: [("attn_w_qkv", (D, 1280)), ("attn_w_o", (D, D))],
    1: [("conv_w_pw1", (D, 2 * D)), ("conv_w_pw2", (D, D))],
    2: [("hgrn_w_qfig", (D, 4 * D)), ("hgrn_w_o", (D, D))],
}
HEADS = 16
HD = 64
SLOPES = [2.0 ** (-8.0 * (i + 1) / HEADS) for i in range(HEADS)]


class Builder:
    def __init__(self, layers=(0, 1, 2, 3), final_norm=True, mixers=True, mlps=True):
        self.layers = list(layers)
        self.final_norm = final_norm
        self.do_mixers = mixers
        self.do_mlps = mlps
        self.nc = bass.Bass("TRN2", target_bir_lowering=False)
        self.S = Sched()
        self.es = ExitStack()
        self._psrr = 0

    def dram_in(self, name, shape):
        return self.nc.dram_tensor(name, list(shape), F32, kind="ExternalInput").ap()

    def sb(self, name, shape, dt):
        return self.es.enter_context(self.nc.sbuf_tensor(name, list(shape), dt))

    def declare(self):
        nc = self.nc
        self.xT = self.dram_in("xT", (D, T))
        self.outT = nc.dram_tensor("outT", [D, T], F32, kind="ExternalOutput").ap()
        self.w = {}
        self.pvec_d = self.dram_in("pvec", (128, PV_N))
        for li in self.layers:
            if self.do_mlps:
                self.w[f"mlp_w1_{li}"] = self.dram_in(f"mlp_w1_{li}", (D, DFF))
                self.w[f"mlp_w2_{li}"] = self.dram_in(f"mlp_w2_{li}", (DFF, D))
            if self.do_mixers:
                for nm, shp in MIXER_W[LAYER_KIND[li]]:
                    self.w[f"{nm}_{li}"] = self.dram_in(f"{nm}_{li}", shp)

        self.h = self.sb("h", (128, NCH, T), F32)
        self.xn = self.sb("xn", (128, NCH, T), BF16)
        self.ARENA_F32 = 26112
        self.arena = self.sb("arena", (128, self.ARENA_F32), F32)
        self.ones_bf = self.sb("ones_bf", (128, 128), BF16)
        self.pvec = self.sb("pvec_sb", (128, PV_N), F32)
        self.gvec = self.pvec[:, PV["gains"][0]:PV["gains"][1]].rearrange("p (i c) -> p i c", c=NCH)
        self.ps = [self.es.enter_context(nc.psum_tensor(f"ps{i}", [128, 512], F32)) for i in range(8)]

    def carve(self, off_bytes, shape, dt):
        n = int(np.prod(shape[1:]))
        esz = _esz(dt)
        assert off_bytes % 4 == 0 and (n * esz) % 4 == 0
        a = off_bytes // 4
        b = a + n * esz // 4
        assert b <= self.ARENA_F32, (off_bytes, shape)
        v = self.arena[:, a:b]
        if dt != F32:
            v = v.bitcast(dt)
        if len(shape) == 3:
            v = v.rearrange("p (a b) -> p a b", a=shape[1])
        elif len(shape) == 4:
            v = v.rearrange("p (a b c) -> p a b c", a=shape[1], b=shape[2])
        return v

    def setup(self):
        S = self.S
        S.op("pool", lambda e: e.memset(self.ones_bf[:], 1.0), writes=[self.ones_bf[:]])
        S.dma("sp", "pv", [(self.pvec[:], self.pvec_d[:, :])], writes=[self.pvec[:]])
        xv = self.xT.rearrange("(c p) t -> p c t", p=128)
        for tb in range(NTB):
            sl = slice(tb * TB, (tb + 1) * TB)
            S.dma("sp", f"x{tb}", [(self.h[:, :, sl], xv[:, :, sl])], writes=[self.h[:, :, sl]])

    def _small_dma(self, pairs, semkey, writes):
        nc = self.nc
        S = self.S
        idx0 = len(S.ops["sp"])
        S.dma("sp", semkey, pairs, writes=writes)
        for o in S.ops["sp"][idx0:]:
            o["slow"] = True

    def psbank(self):
        i = self._psrr
        self._psrr = (self._psrr + 1) % 8
        return self.ps[i]

    def rmsnorm(self, gi, out_fn, scratch_off=0):
        S = self.S
        sq = [self.carve(scratch_off + i * 8192, (128, NCH, TB), BF16) for i in range(3)]
        rs = [self.carve(scratch_off + 24576 + i * 2048, (128, TB), F32) for i in range(2)]

        def square(tb):
            sl = slice(tb * TB, (tb + 1) * TB)
            sqt = sq[tb % 3]
            if tb % 2 == 0:
                S.op("act", lambda e: e.activation(out=sqt[:], in_=self.h[:, :, sl], func=AF.Square),
                     reads=[self.h[:, :, sl]], writes=[sqt[:]])
            else:
                S.op("pool", lambda e: e.tensor_tensor(out=sqt[:], in0=self.h[:, :, sl], in1=self.h[:, :, sl], op=ALU.mult),
                     reads=[self.h[:, :, sl]], writes=[sqt[:]])

        square(0)
        square(1)
        for tb in range(NTB):
            sl = slice(tb * TB, (tb + 1) * TB)
            if tb + 2 < NTB:
                square(tb + 2)
            sqt = sq[tb % 3]
            ps = self.psbank()
            for k in range(NCH):
                S.op("pe", lambda e, ps=ps, sqt=sqt, k=k: e.matmul(ps[:], lhsT=self.ones_bf[:], rhs=sqt[:, k, :],
                                                                     start=(k == 0), stop=(k == NCH - 1)),
                     reads=[self.ones_bf[:], sqt[:, k, :]], writes=[ps[:]])
            r = rs[tb % 2]
            S.op("act", lambda e, ps=ps, r=r: e.activation(out=r[:], in_=ps[:], func=AF.Ln, scale=1.0 / D, bias=self.eps_t[:, 0:1]),
                 reads=[ps[:], self.eps_t[:]], writes=[r[:]])
            S.op("act", lambda e, r=r: e.activation(out=r[:], in_=r[:], func=AF.Exp, scale=-0.5),
                 reads=[r[:]], writes=[r[:]])
            out_fn(tb, sl, r)

    def norm_to_xn(self, gi, scratch_off=0):
        S = self.S

        def out_fn(tb, sl, r):
            for k in range(NCH):
                S.op("dve", lambda e, k=k, sl=sl, r=r: e.scalar_tensor_tensor(
                        out=self.xn[:, k, sl], in0=self.h[:, k, sl], scalar=self.gvec[:, gi, k:k + 1], in1=r[:],
                        op0=ALU.mult, op1=ALU.mult),
                     reads=[self.h[:, k, sl], self.gvec[:, gi, k:k + 1], r[:]], writes=[self.xn[:, k, sl]])
        self.rmsnorm(gi, out_fn, scratch_off)

    def final(self):
        S = self.S
        ov = self.outT.rearrange("(c p) t -> p c t", p=128)

        def out_fn(tb, sl, r):
            for k in range(NCH):
                S.op("dve", lambda e, k=k, sl=sl, r=r: e.scalar_tensor_tensor(
                        out=self.h[:, k, sl], in0=self.h[:, k, sl], scalar=self.gvec[:, 8, k:k + 1], in1=r[:],
                        op0=ALU.mult, op1=ALU.mult),
                     reads=[self.h[:, k, sl], self.gvec[:, 8, k:k + 1], r[:]], writes=[self.h[:, k, sl]])
            S.dma("sp", f"o{tb}", [(ov[:, :, sl], self.h[:, :, sl])], reads=[self.h[:, :, sl]])
        self.rmsnorm(8, out_fn, 0)
        S.wait_all_dma("sp", [f"o{tb}" for tb in range(NTB)])

    def store_h(self):
        S = self.S
        ov = self.outT.rearrange("(c p) t -> p c t", p=128)
        for tb in range(NTB):
            sl = slice(tb * TB, (tb + 1) * TB)
            S.dma("sp", f"o{tb}", [(ov[:, :, sl], self.h[:, :, sl])], reads=[self.h[:, :, sl]])
        S.wait_all_dma("sp", [f"o{tb}" for tb in range(NTB)])

    def attention(self, li):
        S = self.S
        aj = LAYER_J[li]
        QT_OFF, KT_OFF, V_OFF, DM_OFF, WQ_OFF, WO_OFF, EX_OFF, PT_OFF, RD_OFF = (
            0, 32768, 40960, 45056, 53248, 75776, 92160, 96256, 100352)
        qT = self.carve(QT_OFF, (128, NCH, T), BF16)
        kT = self.carve(KT_OFF, (128, 2, T), BF16)
        vtm = self.carve(V_OFF, (128, 16, 128), BF16)
        DM = self.carve(DM_OFF, (128, HEADS, 256), BF16)
        wq = self.carve(WQ_OFF, (128, NCH, 1408), BF16)
        wo = self.carve(WO_OFF, (128, NCH, D), BF16)
        exs = [self.carve(EX_OFF + i * 2048, (128, 512), F32) for i in range(2)]
        pts = [self.carve(PT_OFF + i * 1024, (128, 512), BF16) for i in range(2)]
        rds = [self.carve(RD_OFF + i * 2048, (128, 512), F32) for i in range(2)]
        wqd = self.w[f"attn_w_qkv_{li}"].rearrange("(k p) f -> p k f", p=128)
        wod = self.w[f"attn_w_o_{li}"].rearrange("(k p) f -> p k f", p=128)
        S.dma("pool", "aq", [(wq[:, :, 0:1024], wqd[:, :, 0:1024]),
                             (wq[:, :, 1024:1088], wqd[:, :, 1024:1088]), (wq[:, :, 1088:1152], wqd[:, :, 1024:1088]),
                             (wq[:, :, 1152:1216], wqd[:, :, 1088:1152]), (wq[:, :, 1216:1280], wqd[:, :, 1088:1152]),
                             (wq[:, :, 1280:1408], wqd[:, :, 1152:1280])], writes=[wq[:]])
        S.dma("pool", "ao", [(wo[:], wod)], writes=[wo[:]])
        self.norm_to_xn(li, scratch_off=QT_OFF)
        import os
        stop = int(os.environ.get("ATT_STOP", "9"))
        if stop <= 1:
            return

        di = self.iota_i
        df = self.carve(EX_OFF + 1024, (128, 256), F32)
        m1 = self.carve(EX_OFF + 2048, (128, 256), F32)
        m2 = self.carve(EX_OFF + 3072, (128, 256), F32)
        tmp = [self.carve(PT_OFF + i * 1024, (128, 256), F32) for i in range(2)]
        S.op("pool", lambda e: e.iota(di[:], pattern=[[-128, 2], [1, 128]], base=128, channel_multiplier=-1), writes=[di[:]])
        S.op("dve", lambda e: e.tensor_copy(out=df[:], in_=di[:]), reads=[di[:]], writes=[df[:]])
        S.op("dve", lambda e: e.tensor_single_scalar(out=m1[:], in_=df[:], scalar=0.0, op=ALU.is_ge), reads=[df[:]], writes=[m1[:]])
        S.op("dve", lambda e: e.tensor_single_scalar(out=m2[:], in_=df[:], scalar=127.0, op=ALU.is_le), reads=[df[:]], writes=[m2[:]])
        S.op("dve", lambda e: e.tensor_tensor(out=m1[:], in0=m1[:], in1=m2[:], op=ALU.mult), reads=[m1[:], m2[:]], writes=[m1[:]])
        S.op("dve", lambda e: e.tensor_scalar(out=df[:], in0=df[:], scalar1=0.0, scalar2=128.0, op0=ALU.max, op1=ALU.min),
             reads=[df[:]], writes=[df[:]])
        for hd in range(HEADS):
            t = tmp[hd % 2]
            S.op("act", lambda e, t=t, hd=hd: e.activation(out=t[:], in_=df[:], func=AF.Exp, scale=-SLOPES[hd]),
                 reads=[df[:]], writes=[t[:]])
            S.op("dve", lambda e, t=t, hd=hd: e.tensor_tensor(out=DM[:, hd, :], in0=t[:], in1=m1[:], op=ALU.mult),
                 reads=[t[:], m1[:]], writes=[DM[:, hd, :]])
        es = self.es_t[:, aj * NCH:(aj + 1) * NCH]
        c0 = PV["sinks"][0] + aj * NCH
        S.op("act", lambda e: e.activation(out=es, in_=self.pvec[:, c0:c0 + NCH], func=AF.Exp),
             reads=[self.pvec[:, c0:c0 + NCH]], writes=[es])

        if stop <= 2:
            return
        cnt = [0]

        def evac(dst, ps_ap):
            if cnt[0] % 2 == 0:
                S.op("act", lambda e: e.activation(out=dst, in_=ps_ap, func=AF.Copy), reads=[ps_ap], writes=[dst])
            else:
                S.op("dve", lambda e: e.tensor_copy(out=dst, in_=ps_ap), reads=[ps_ap], writes=[dst])
            cnt[0] += 1

        for cj in range(NCH + 2):
            for tb in range(NTB):
                sl = slice(tb * TB, (tb + 1) * TB)
                ps = self.psbank()
                for k in range(NCH):
                    S.op("pe", lambda e, ps=ps, k=k, cj=cj, sl=sl: e.matmul(
                            ps[:], lhsT=wq[:, k, cj * 128:(cj + 1) * 128], rhs=self.xn[:, k, sl],
                            start=(k == 0), stop=(k == NCH - 1)),
                         reads=[wq[:, k, cj * 128:(cj + 1) * 128], self.xn[:, k, sl]], writes=[ps[:]])
                dst = qT[:, cj, sl] if cj < NCH else kT[:, cj - NCH, sl]
                evac(dst, ps[:])
        for g in range(4):
            ps = self.psbank()
            for ii in range(4):
                i = g * 4 + ii
                for k in range(NCH):
                    S.op("pe", lambda e, ps=ps, k=k, i=i, ii=ii: e.matmul(
                            ps[:, ii * 128:(ii + 1) * 128], lhsT=self.xn[:, k, i * 128:(i + 1) * 128], rhs=wq[:, k, 1280:1408],
                            start=(k == 0), stop=(k == NCH - 1)),
                         reads=[self.xn[:, k, i * 128:(i + 1) * 128], wq[:, k, 1280:1408]], writes=[ps[:, ii * 128:(ii + 1) * 128]])
            evac(vtm[:, g * 4:(g + 1) * 4, :], ps[:].rearrange("p (a b) -> p a b", a=4))

        if stop <= 3:
            return
        aoT = self.xn
        core = int(os.environ.get("ATT_CORE", "9"))
        items = [(j, q) for j in range(NCH) for q in range(8)]
        sbanks = [[self.ps[0], self.ps[1]], [self.ps[2], self.ps[3]]]
        nbanks = [self.ps[4], self.ps[5]]
        dbanks = [self.ps[6], self.ps[7]]
        exs = [[self.carve(EX_OFF + (i * 2 + hh) * 1024, (128, 512), BF16) for hh in range(2)] for i in range(2)]
        pts = [[self.carve(PT_OFF + (i * 2 + hh) * 1024, (128, 512), BF16) for hh in range(2)] for i in range(2)]

        def v4(ap):
            return ap.rearrange("p (a b c) -> p a b c", a=2, b=2)

        def scores(ix):
            j, q = items[ix]
            kv = j // 4
            for hh in range(2):
                ps = sbanks[ix % 2][hh]
                psv = v4(ps[:])
                pr = slice(hh * 64, (hh + 1) * 64)
                for nn in range(2):
                    n = 2 * q + nn
                    for mb in range(2):
                        m = max(n - 1 + mb, 0)
                        S.op("pe", lambda e, psv=psv, nn=nn, mb=mb, m=m, pr=pr, n=n: e.matmul(
                                psv[:, nn, mb, :], lhsT=kT[pr, kv, m * 128:(m + 1) * 128], rhs=qT[pr, j, n * 128:(n + 1) * 128],
                                start=True, stop=True),
                             reads=[kT[pr, kv, m * 128:(m + 1) * 128], qT[pr, j, n * 128:(n + 1) * 128]],
                             writes=[psv[:, nn, mb, :]])
                if core == 0:
                    continue
                ex = exs[ix % 2][hh]
                pt = pts[ix % 2][hh]
                S.op("act", lambda e, ex=ex, ps=ps: e.activation(out=ex[:], in_=ps[:], func=AF.Exp, scale=HD ** -0.5),
                     reads=[ps[:]], writes=[ex[:]])
                for nn in range(2):
                    cs = slice(nn * 256, (nn + 1) * 256)
                    S.op("pool" if hh == 0 else "dve", lambda e, ex=ex, pt=pt, cs=cs, hh=hh: e.tensor_tensor(out=pt[:, cs], in0=ex[:, cs], in1=DM[:, 2 * j + hh, :], op=ALU.mult),
                         reads=[ex[:, cs], DM[:, 2 * j + hh, :]], writes=[pt[:, cs]])

        def pv(ix):
            j, q = items[ix]
            kv = j // 4
            g = ix // 2
            psN = nbanks[g % 2]
            psD = dbanks[g % 2]
            for nn in range(2):
                n = 2 * q + nn
                cs = slice((n % 4) * 128, (n % 4 + 1) * 128)
                for hh in range(2):
                    ptv = v4(pts[ix % 2][hh][:])
                    pr = slice(hh * 64, (hh + 1) * 64)
                    mlist = [mb for mb in range(2) if n - 1 + mb >= 0]
                    for (dst, which) in ((psN, 0), (psD, 1)):
                        for mb in mlist:
                            m = n - 1 + mb
                            lhs = vtm[:, m, kv * 64:(kv + 1) * 64] if which == 0 else self.ones_bf[:, 0:64]
                            S.op("pe", lambda e, dst=dst, pr=pr, cs=cs, lhs=lhs, ptv=ptv, nn=nn, mb=mb, mlist=mlist: e.matmul(
                                    dst[pr, cs], lhsT=lhs, rhs=ptv[:, nn, mb, :],
                                    start=(mb == mlist[0]), stop=(mb == mlist[-1])),
                                 reads=[lhs, ptv[:, nn, mb, :]], writes=[dst[pr, cs]])
        def normalize(ix):
            j, q = items[ix]
            g = ix // 2
            psN = nbanks[g % 2]
            psD = dbanks[g % 2]
            if True:
                rd = rds[g % 2]
                osl = slice((q - 1) * 256, (q + 1) * 256)
                S.op("act", lambda e: e.activation(out=rd[:], in_=psD[:], func=AF.Ln, bias=es[:, j:j + 1]),
                     reads=[psD[:], es[:, j:j + 1]], writes=[rd[:]])
                S.op("act", lambda e: e.activation(out=rd[:], in_=rd[:], func=AF.Exp, scale=-1.0), reads=[rd[:]], writes=[rd[:]])
                S.op("dve", lambda e: e.tensor_tensor(out=aoT[:, j, osl], in0=psN[:], in1=rd[:], op=ALU.mult),
                     reads=[psN[:], rd[:]], writes=[aoT[:, j, osl]])

        scores(0)
        for ix in range(len(items)):
            if ix + 1 < len(items):
                scores(ix + 1)
            if core >= 2:
                pv(ix)
                if ix >= 1 and (ix - 1) % 2 == 1:
                    normalize(ix - 1)
        normalize(len(items) - 1)

        if stop <= 4:
            return
        for tb in range(NTB):
            for m in range(NCH):
                sl = slice(tb * TB, (tb + 1) * TB)
                ps = self.ps[(tb * NCH + m) % 8]
                for j in range(NCH):
                    S.op("pe", lambda e, ps=ps, m=m, j=j, sl=sl: e.matmul(
                            ps[:], lhsT=wo[:, j, m * 128:(m + 1) * 128], rhs=aoT[:, j, sl],
                            start=(j == 0), stop=(j == NCH - 1)),
                         reads=[wo[:, j, m * 128:(m + 1) * 128], aoT[:, j, sl]], writes=[ps[:]])
                S.op("dve", lambda e, ps=ps, m=m, sl=sl: e.tensor_tensor(out=self.h[:, m, sl], in0=self.h[:, m, sl], in1=ps[:], op=ALU.add),
                     reads=[self.h[:, m, sl], ps[:]], writes=[self.h[:, m, sl]])

    def conv(self, li):
        S = self.S
        PAD = 32
        U_OFF, W1_OFF, W2_OFF, DG_OFF, SG_OFF = 0, 33280, 66048, 82432, 98304
        ub = self.carve(U_OFF, (128, NCH, T + PAD), BF16)
        w1c = self.carve(W1_OFF, (128, NCH, 2 * D), BF16)
        cbf = self.carve(W1_OFF, (128, NCH, T), BF16)
        w2c = self.carve(W2_OFF, (128, NCH, D), BF16)
        dg = [self.carve(DG_OFF + i * 256, (128, 128), BF16) for i in range(62)]
        sg = [self.carve(SG_OFF + i * 2048, (128, TB), F32) for i in range(2)]
        pvc = lambda nm, i: self.pvec[:, PV[nm][0] + i:PV[nm][0] + i + 1]
        S.dma("pool", "cw1", [(w1c[:], self.w[f"conv_w_pw1_{li}"].rearrange("(k p) f -> p k f", p=128))], writes=[w1c[:]])
        S.dma("pool", "cw2", [(w2c[:], self.w[f"conv_w_pw2_{li}"].rearrange("(k p) f -> p k f", p=128))], writes=[w2c[:]])
        self.norm_to_xn(li, scratch_off=U_OFF)
        S.op("pool", lambda e: e.memset(ub[:, :, 0:PAD], 0.0), writes=[ub[:, :, 0:PAD]])
        import os
        stop = int(os.environ.get("CONV_STOP", "9"))
        if stop <= 1:
            return
        ix = 0
        for j in range(NCH):
            for tb in range(NTB):
                sl = slice(tb * TB, (tb + 1) * TB)
                psA = self.ps[(2 * ix) % 8]
                psG = self.ps[(2 * ix + 1) % 8]
                for (ps, c0) in ((psG, D + j * 128), (psA, j * 128)):
                    for k in range(NCH):
                        S.op("pe", lambda e, ps=ps, k=k, c0=c0, sl=sl: e.matmul(
                                ps[:], lhsT=w1c[:, k, c0:c0 + 128], rhs=self.xn[:, k, sl], start=(k == 0), stop=(k == NCH - 1)),
                             reads=[w1c[:, k, c0:c0 + 128], self.xn[:, k, sl]], writes=[ps[:]])
                sgt = sg[ix % 2]
                S.op("act", lambda e, psG=psG, sgt=sgt, j=j: e.activation(out=sgt[:], in_=psG[:], func=AF.Sigmoid, bias=pvc("cb1", NCH + j)),
                     reads=[psG[:]], writes=[sgt[:]])
                S.op("dve", lambda e, psA=psA, sgt=sgt, j=j, sl=sl: e.scalar_tensor_tensor(
                        out=ub[:, j, PAD + sl.start:PAD + sl.stop], in0=psA[:], scalar=pvc("cb1", j), in1=sgt[:], op0=ALU.add, op1=ALU.mult),
                     reads=[psA[:], sgt[:]], writes=[ub[:, j, PAD + sl.start:PAD + sl.stop]])
                ix += 1
        if stop <= 2:
            return
        for j in range(NCH):
            dgs = dg[(j % 2) * 31:(j % 2) * 31 + 31]
            for w in range(31):
                S.op("dve", lambda e, w=w, j=j, dgs=dgs: e.tensor_scalar(out=dgs[w][:], in0=self.ident_bf[:], scalar1=pvc("cdw", w * NCH + j),
                                                                         scalar2=None, op0=ALU.mult),
                     reads=[self.ident_bf[:]], writes=[dgs[w][:]])
            for tb in range(NTB):
                sl = slice(tb * TB, (tb + 1) * TB)
                ps = self.ps[(j * NTB + tb) % 8]
                for w in range(31):
                    c0 = tb * TB + (PAD - 30) + w
                    S.op("pe", lambda e, ps=ps, w=w, j=j, c0=c0, dgs=dgs: e.matmul(
                            ps[:], lhsT=dgs[w][:], rhs=ub[:, j, c0:c0 + TB], start=(w == 0), stop=(w == 30)),
                         reads=[dgs[w][:], ub[:, j, c0:c0 + TB]], writes=[ps[:]])
                if (j * NTB + tb) % 2 == 0:
                    S.op("act", lambda e, ps=ps, j=j, sl=sl: e.activation(out=cbf[:, j, sl], in_=ps[:], func=AF.Identity, bias=pvc("cbd", j)),
                         reads=[ps[:]], writes=[cbf[:, j, sl]])
                else:
                    S.op("dve", lambda e, ps=ps, j=j, sl=sl: e.tensor_scalar(out=cbf[:, j, sl], in0=ps[:], scalar1=pvc("cbd", j), scalar2=None, op0=ALU.add),
                         reads=[ps[:]], writes=[cbf[:, j, sl]])
        if stop <= 3:
            return
        z = self.xn
        csqs = [self.carve(U_OFF + i * 8192, (128, NCH, TB), BF16) for i in range(2)]
        sts = [[self.carve(U_OFF + 16384 + s_ * 8192 + i * 2048, (128, TB), F32) for i in range(4)] for s_ in range(2)]
        tt = [self.carve(DG_OFF + i * 2048, (128, TB), F32) for i in range(4)]

        def ln_stats(tb):
            sl = slice(tb * TB, (tb + 1) * TB)
            csq = csqs[tb % 2]
            mean, msq, rstd, nmr = sts[tb % 2]
            if tb % 2 == 0:
                S.op("act", lambda e: e.activation(out=csq[:], in_=cbf[:, :, sl], func=AF.Square), reads=[cbf[:, :, sl]], writes=[csq[:]])
            else:
                S.op("pool", lambda e: e.tensor_tensor(out=csq[:], in0=cbf[:, :, sl], in1=cbf[:, :, sl], op=ALU.mult),
                     reads=[cbf[:, :, sl]], writes=[csq[:]])
            psM = self.ps[(2 * tb) % 8]
            psQ = self.ps[(2 * tb + 1) % 8]
            for j in range(NCH):
                S.op("pe", lambda e, j=j: e.matmul(psM[:], lhsT=self.ones_bf[:], rhs=cbf[:, j, sl], start=(j == 0), stop=(j == NCH - 1)),
                     reads=[cbf[:, j, sl]], writes=[psM[:]])
            for j in range(NCH):
                S.op("pe", lambda e, j=j: e.matmul(psQ[:], lhsT=self.ones_bf[:], rhs=csq[:, j, :], start=(j == 0), stop=(j == NCH - 1)),
                     reads=[csq[:, j, :]], writes=[psQ[:]])
            S.op("act", lambda e: e.activation(out=mean[:], in_=psM[:], func=AF.Copy, scale=1.0 / D), reads=[psM[:]], writes=[mean[:]])
            S.op("dve", lambda e: e.tensor_tensor(out=msq[:], in0=mean[:], in1=mean[:], op=ALU.mult), reads=[mean[:]], writes=[msq[:]])
            S.op("dve", lambda e: e.scalar_tensor_tensor(out=rstd[:], in0=psQ[:], scalar=1.0 / D, in1=msq[:], op0=ALU.mult, op1=ALU.subtract),
                 reads=[psQ[:], msq[:]], writes=[rstd[:]])
            S.op("act", lambda e: e.activation(out=rstd[:], in_=rstd[:], func=AF.Ln, bias=self.eps_t[:, 0:1]), reads=[rstd[:]], writes=[rstd[:]])
            S.op("act", lambda e: e.activation(out=rstd[:], in_=rstd[:], func=AF.Exp, scale=-0.5), reads=[rstd[:]], writes=[rstd[:]])
            S.op("dve", lambda e: e.scalar_tensor_tensor(out=nmr[:], in0=mean[:], scalar=-1.0, in1=rstd[:], op0=ALU.mult, op1=ALU.mult),
                 reads=[mean[:], rstd[:]], writes=[nmr[:]])

        def ln_apply(tb):
            sl = slice(tb * TB, (tb + 1) * TB)
            mean, msq, rstd, nmr = sts[tb % 2]
            for j in range(NCH):
                t = tt[j % 4]
                S.op("dve", lambda e, t=t, j=j: e.tensor_tensor(out=t[:], in0=cbf[:, j, sl], in1=rstd[:], op=ALU.mult),
                     reads=[cbf[:, j, sl], rstd[:]], writes=[t[:]])
                S.op("dve" if j % 2 == 0 else "pool", lambda e, t=t: e.tensor_tensor(out=t[:], in0=t[:], in1=nmr[:], op=ALU.add),
                     reads=[t[:], nmr[:]], writes=[t[:]])
                S.op("act", lambda e, t=t, j=j: e.activation(out=z[:, j, sl], in_=t[:], func=AF.Silu, scale=pvc("clg", j), bias=pvc("clb", j)),
                     reads=[t[:]], writes=[z[:, j, sl]])

        def pw2(tb):
            sl = slice(tb * TB, (tb + 1) * TB)
            for m in range(NCH):
                ps = self.ps[(tb * NCH + m) % 8]
                for j in range(NCH):
                    S.op("pe", lambda e, ps=ps, m=m, j=j: e.matmul(
                            ps[:], lhsT=w2c[:, j, m * 128:(m + 1) * 128], rhs=z[:, j, sl], start=(j == 0), stop=(j == NCH - 1)),
                         reads=[w2c[:, j, m * 128:(m + 1) * 128], z[:, j, sl]], writes=[ps[:]])
                S.op("dve", lambda e, ps=ps, m=m: e.scalar_tensor_tensor(
                        out=self.h[:, m, sl], in0=ps[:], scalar=pvc("cb2", m), in1=self.h[:, m, sl], op0=ALU.add, op1=ALU.add),
                     reads=[ps[:], self.h[:, m, sl]], writes=[self.h[:, m, sl]])

        ln_stats(0)
        for tb in range(NTB):
            if tb + 1 < NTB:
                ln_stats(tb + 1)
            ln_apply(tb)
            pw2(tb)

    def hgrn(self, li):
        S = self.S
        import os
        stop = int(os.environ.get("HG_STOP", "9"))
        Z_OFF, QT_OFF, KT_OFF, V_OFF, WS_OFF, SC_OFF = 0, 32768, 49152, 65536, 81920, 98304
        z = self.carve(Z_OFF, (128, NCH, T), BF16)
        qt = self.carve(QT_OFF, (128, 4, T), BF16)
        kt = self.carve(KT_OFF, (128, 4, T), BF16)
        vtm = self.carve(V_OFF, (128, 16, 512), BF16)
        wsl = [self.carve(WS_OFF + i * 8192, (128, NCH, 512), BF16) for i in range(2)]
        wd = self.w[f"hgrn_w_qfig_{li}"].rearrange("(k p) f -> p k f", p=128)
        pvc = lambda nm, i: self.pvec[:, PV[nm][0] + i:PV[nm][0] + i + 1]

        def load(slot, grp):
            S.dma("pool", f"hw{slot}", [(wsl[slot][:], wd[:, :, grp * 512:(grp + 1) * 512])], writes=[wsl[slot][:]])

        msk = self.carve(SC_OFF, (128, TB), BF16)
        mk4 = self.carve(SC_OFF + 1024, (128, 4, 128), BF16)
        ebl = self.carve(SC_OFF + 2048, (128, 4, 32), F32)
        le = self.carve(SC_OFF + 2560, (128, 4, NCH), F32)
        se = self.carve(SC_OFF + 2688, (128, NCH), F32)
        lbv = self.carve(SC_OFF + 2720, (128, NCH), F32)
        oml = self.carve(SC_OFF + 2752, (128, NCH), F32)
        mi = self.iota_i[:, 0:128]
        mf = self.carve(SC_OFF + 3584, (128, 128), F32)

        load(0, 0)
        load(1, 2)
        self.norm_to_xn(li, scratch_off=Z_OFF)
        c0 = PV["hlb"][0]
        S.op("act", lambda e: e.activation(out=le[:].rearrange("p a b -> p (a b)"), in_=self.pvec[:, c0:c0 + 4 * NCH], func=AF.Exp),
             reads=[self.pvec[:, c0:c0 + 4 * NCH]], writes=[le[:]])
        S.op("dve", lambda e: e.tensor_tensor(out=se[:], in0=le[:, 0, :], in1=le[:, 1, :], op=ALU.add), reads=[le[:]], writes=[se[:]])
        S.op("dve", lambda e: e.tensor_tensor(out=se[:], in0=se[:], in1=le[:, 2, :], op=ALU.add), reads=[le[:], se[:]], writes=[se[:]])
        S.op("dve", lambda e: e.tensor_tensor(out=se[:], in0=se[:], in1=le[:, 3, :], op=ALU.add), reads=[le[:], se[:]], writes=[se[:]])
        S.op("dve", lambda e: e.reciprocal(out=se[:], in_=se[:]), reads=[se[:]], writes=[se[:]])
        if li == 0:
            S.op("dve", lambda e: e.memset(lbv[:], 0.0), writes=[lbv[:]])
        else:
            S.op("dve", lambda e: e.tensor_copy(out=lbv[:], in_=le[:, 1, :]), reads=[le[:]], writes=[lbv[:]])
            for i in range(2, li + 1):
                S.op("dve", lambda e, i=i: e.tensor_tensor(out=lbv[:], in0=lbv[:], in1=le[:, i, :], op=ALU.add), reads=[le[:], lbv[:]], writes=[lbv[:]])
        S.op("dve", lambda e: e.tensor_tensor(out=lbv[:], in0=lbv[:], in1=se[:], op=ALU.mult), reads=[lbv[:], se[:]], writes=[lbv[:]])
        S.op("dve", lambda e: e.tensor_scalar(out=oml[:], in0=lbv[:], scalar1=-1.0, scalar2=1.0, op0=ALU.mult, op1=ALU.add),
             reads=[lbv[:]], writes=[oml[:]])
        S.op("pool", lambda e: e.memset(msk[:], 1.0), writes=[msk[:]])
        mskv = msk[:].rearrange("p (c t) -> p c t", t=64)
        S.op("pool", lambda e: e.memset(mskv[:, :, 0:1], 0.0), reads=[msk[:]], writes=[msk[:]])
        S.op("pool", lambda e: e.iota(mi[:], pattern=[[1, 128]], base=0, channel_multiplier=-1), writes=[mi[:]])
        S.op("dve", lambda e: e.tensor_copy(out=mf[:], in_=mi[:]), reads=[mi[:]], writes=[mf[:]])
        S.op("dve", lambda e: e.tensor_single_scalar(out=mf[:], in_=mf[:], scalar=0.0, op=ALU.is_ge), reads=[mf[:]], writes=[mf[:]])
        S.op("dve", lambda e: e.memset(mf[0:64, 64:128], 0.0), reads=[mf[:]], writes=[mf[:]])
        for hd in range(4):
            S.op("dve", lambda e, hd=hd: e.tensor_copy(out=mk4[:, hd, :], in_=mf[:]), reads=[mf[:]], writes=[mk4[:, hd, :]])
        if stop <= 1:
            return

        T1 = self.carve(Z_OFF + 16384, (128, T), F32)
        T2 = self.carve(Z_OFF + 24576, (128, T), F32)
        T3 = self.carve(V_OFF, (128, T), F32)
        T4 = [self.carve(V_OFF + 8192 + i * 4096, (128, T), BF16) for i in range(2)]
        QS = 128 ** -0.5

        for hf in range(2):
            if hf == 1:
                load(0, 1)
                load(1, 3)
            for hd in range(4):
                gh = hf * 4 + hd
                t4 = T4[hd % 2]
                for slot in (1, 0):
                    for tb in range(NTB):
                        sl = slice(tb * TB, (tb + 1) * TB)
                        ps = self.ps[2 * tb] if slot == 1 else self.ps[2 * tb + 1]
                        for k in range(NCH):
                            S.op("pe", lambda e, ps=ps, slot=slot, k=k, hd=hd, sl=sl: e.matmul(
                                    ps[:], lhsT=wsl[slot][:, k, hd * 128:(hd + 1) * 128], rhs=self.xn[:, k, sl],
                                    start=(k == 0), stop=(k == NCH - 1)),
                                 reads=[wsl[slot][:, k, hd * 128:(hd + 1) * 128], self.xn[:, k, sl]], writes=[ps[:]])
                    if hd == 3 and slot == 1:
                        load(1, 4 + hf)
                for tb in range(NTB):
                    sl = slice(tb * TB, (tb + 1) * TB)
                    psF = self.ps[2 * tb]
                    S.op("act", lambda e, psF=psF, sl=sl: e.activation(out=T1[:, sl], in_=psF[:], func=AF.Sigmoid), reads=[psF[:]], writes=[T1[:, sl]])
                for tb in range(NTB):
                    sl = slice(tb * TB, (tb + 1) * TB)
                    psQ = self.ps[2 * tb + 1]
                    S.op("act", lambda e, psQ=psQ, sl=sl, t4=t4: e.activation(out=t4[:, sl], in_=psQ[:], func=AF.Silu), reads=[psQ[:]], writes=[t4[:, sl]])
                S.op("dve", lambda e, gh=gh: e.tensor_scalar(out=T1[:], in0=T1[:], scalar1=oml[:, gh:gh + 1], scalar2=lbv[:, gh:gh + 1],
                                                             op0=ALU.mult, op1=ALU.add),
                     reads=[T1[:], oml[:], lbv[:]], writes=[T1[:]])
                S.op("act", lambda e: e.activation(out=T2[:], in_=T1[:], func=AF.Ln), reads=[T1[:]], writes=[T2[:]])
                for tb in range(NTB):
                    sl = slice(tb * TB, (tb + 1) * TB)
                    S.op("dve", lambda e, sl=sl: e.tensor_tensor_scan(out=T3[:, sl], data0=msk[:], data1=T2[:, sl], initial=0.0, op0=ALU.mult, op1=ALU.add),
                         reads=[msk[:], T2[:, sl]], writes=[T3[:, sl]])
                S.op("act", lambda e: e.activation(out=T2[:], in_=T3[:], func=AF.Exp, scale=-1.0), reads=[T3[:]], writes=[T2[:]])
                S.op("pool", lambda e: e.tensor_scalar(out=T1[:], in0=T1[:], scalar1=-1.0, scalar2=1.0, op0=ALU.mult, op1=ALU.add),
                     reads=[T1[:]], writes=[T1[:]])
                S.op("dve", lambda e, hd=hd: e.tensor_tensor(out=kt[:, hd, :], in0=T1[:], in1=T2[:], op=ALU.mult),
                     reads=[T1[:], T2[:]], writes=[kt[:, hd, :]])
                S.op("act", lambda e: e.activation(out=T2[:], in_=T3[:], func=AF.Exp), reads=[T3[:]], writes=[T2[:]])
                ebv = T2[:].rearrange("p (c t) -> p c t", t=64)
                S.op("pool", lambda e, hd=hd, ebv=ebv: e.tensor_copy(out=ebl[:, hd, :], in_=ebv[:, :, 63]), reads=[T2[:]], writes=[ebl[:, hd, :]])
                S.op("dve", lambda e, hd=hd, t4=t4: e.scalar_tensor_tensor(out=qt[:, hd, :], in0=t4[:], scalar=QS, in1=T2[:], op0=ALU.mult, op1=ALU.mult),
                     reads=[t4[:], T2[:]], writes=[qt[:, hd, :]])
                if os.environ.get("HG_DBG") == "1":
                    ov = self.outT.rearrange("(c p) t -> p c t", p=128)
                    S.dma("sp", "dbg", [(ov[:, 0, :], T1[:]), (ov[:, 1, :], T2[:]), (ov[:, 2, :], T3[:])], reads=[T1[:], T2[:], T3[:]])
                    S.wait_all_dma("sp", ["dbg"])
                    self.dbg_stop = True
                    return
            for i in range(16):
                ps = self.ps[i % 8]
                for k in range(NCH):
                    S.op("pe", lambda e, ps=ps, k=k, i=i: e.matmul(ps[:], lhsT=self.xn[:, k, i * 128:(i + 1) * 128], rhs=wsl[1][:, k, :],
                                                                   start=(k == 0), stop=(k == NCH - 1)),
                         reads=[self.xn[:, k, i * 128:(i + 1) * 128], wsl[1][:, k, :]], writes=[ps[:]])
                S.op("dve", lambda e, ps=ps, i=i: e.tensor_copy(out=vtm[:, i, :], in_=ps[:]), reads=[ps[:]], writes=[vtm[:, i, :]])
            if os.environ.get("HG_DBG") == "2":
                ov = self.outT.rearrange("(c p) t -> p c t", p=128)
                S.dma("pool", "dbg", [(ov[:, 0:4, :], kt[:]), (ov[:, 4:8, :], qt[:])], reads=[kt[:], qt[:]])
                S.wait_all_dma("pool", ["dbg"])
                self.dbg_stop = True
                return
            if stop <= 2:
                continue
            def _recur(hf, R):
                At = [self.carve(R + i * 1024, (128, 4, 128), BF16) for i in range(2)]
                ktm = [self.carve(R + 2048 + i * 1024, (128, 4, 128), BF16) for i in range(2)]
                Sbf = [self.carve(R + 4096 + i * 1024, (128, 4, 128), BF16) for i in range(2)]
                Sf = self.carve(R + 6144, (128, 4, 128), F32)
                tmpS = [self.carve(R + 8192 + i * 2048, (128, 4, 128), F32) for i in range(2)]
                osq = self.carve(R + 12288, (128, 512), BF16)
                rstd = self.carve(R + 13312, (128, 4, 128), F32)

                Sbf = [self.carve(R + 4096 + i * 1024, (128, 4, 128), BF16) for i in range(2)] + \
                      [self.carve(R + 15360, (128, 4, 128), BF16), self.carve(SC_OFF + 4096, (128, 4, 128), BF16)]
                flat = lambda t: t[:].rearrange("p a b -> p (a b)")

                def pre(i):
                    tl = slice(i * 128, (i + 1) * 128)
                    psA = self.ps[0]
                    psK = self.ps[1]
                    for hd in range(4):
                        cs = slice(hd * 128, (hd + 1) * 128)
                        S.op("pe", lambda e, psA=psA, hd=hd, cs=cs, tl=tl: e.matmul(psA[:, cs], lhsT=kt[:, hd, tl], rhs=qt[:, hd, tl], start=True, stop=True),
                             reads=[kt[:, hd, tl], qt[:, hd, tl]], writes=[psA[:, cs]])
                    S.op("dve", lambda e, psA=psA, i=i: e.tensor_tensor(out=flat(At[i % 2]), in0=psA[:], in1=flat(mk4), op=ALU.mult),
                         reads=[psA[:], mk4[:]], writes=[At[i % 2][:]])
                    for hd in range(4):
                        cs = slice(hd * 128, (hd + 1) * 128)
                        S.op("pe", lambda e, psK=psK, hd=hd, cs=cs, tl=tl: e.matmul(psK[:, cs], lhsT=kt[:, hd, tl], rhs=self.ident_bf[:], start=True, stop=True),
                             reads=[kt[:, hd, tl], self.ident_bf[:]], writes=[psK[:, cs]])
                    S.op("dve", lambda e, psK=psK, i=i: e.tensor_copy(out=flat(ktm[i % 2]), in_=psK[:]),
                         reads=[psK[:]], writes=[ktm[i % 2][:]])

                def stateP(i, cc):
                    if 2 * i + cc == 31:
                        return
                    psS = self.ps[4 + cc]
                    pr = slice(cc * 64, (cc + 1) * 64)
                    for hd in range(4):
                        cs = slice(hd * 128, (hd + 1) * 128)
                        S.op("pe", lambda e, psS=psS, hd=hd, cs=cs, pr=pr, i=i: e.matmul(psS[:, cs], lhsT=ktm[i % 2][pr, hd, :], rhs=vtm[pr, i, cs],
                                                                                       start=True, stop=True),
                             reads=[ktm[i % 2][pr, hd, :], vtm[pr, i, cs]], writes=[psS[:, cs]])

                U = [self.carve(R + 6144, (128, 4, 128), F32), self.carve(R + 8192, (128, 4, 128), F32)]
                S.op("dve", lambda e: e.memset(U[1][:], 0.0), writes=[U[1][:]])

                def chain(i, cc):
                    c = 2 * i + cc
                    if c == 31:
                        return
                    psS = self.ps[4 + cc]
                    cp = max(c - 1, 0)
                    for hd in range(4):
                        cs = slice(hd * 128, (hd + 1) * 128)
                        S.op("dve", lambda e, hd=hd, cs=cs, c=c, cp=cp, psS=psS: e.scalar_tensor_tensor(
                                out=U[c % 2][:, hd, :], in0=U[(c + 1) % 2][:, hd, :], scalar=ebl[:, hd, cp:cp + 1], in1=psS[:, cs],
                                op0=ALU.mult, op1=ALU.add),
                             reads=[U[(c + 1) % 2][:, hd, :], ebl[:, hd, cp:cp + 1], psS[:, cs]], writes=[U[c % 2][:, hd, :]])
                    for hd in range(4):
                        if hd < 1:
                            S.op("act", lambda e, hd=hd, c=c: e.activation(out=Sbf[c % 4][:, hd, :], in_=U[c % 2][:, hd, :], func=AF.Copy,
                                                                           scale=ebl[:, hd, c:c + 1]),
                                 reads=[U[c % 2][:, hd, :], ebl[:, hd, c:c + 1]], writes=[Sbf[c % 4][:, hd, :]])
                        else:
                            S.op("pool", lambda e, hd=hd, c=c: e.tensor_scalar(out=Sbf[c % 4][:, hd, :], in0=U[c % 2][:, hd, :],
                                                                               scalar1=ebl[:, hd, c:c + 1], scalar2=1.0, op0=ALU.mult, op1=ALU.mult),
                                 reads=[U[c % 2][:, hd, :], ebl[:, hd, c:c + 1]], writes=[Sbf[c % 4][:, hd, :]])

                def omm(i):
                    psO = self.ps[2 + i % 2]
                    for hd in range(4):
                        cs = slice(hd * 128, (hd + 1) * 128)
                        mms = [(psO[:, cs], vtm[:, i, cs], At[i % 2][:, hd, :])]
                        if i > 0:
                            mms.append((psO[:, hd * 128:hd * 128 + 64], Sbf[(2 * i - 1) % 4][:, hd, :], qt[:, hd, i * 128:i * 128 + 64]))
                        mms.append((psO[:, hd * 128 + 64:hd * 128 + 128], Sbf[(2 * i) % 4][:, hd, :], qt[:, hd, i * 128 + 64:i * 128 + 128]))
                        for n_, (o_, l_, r_) in enumerate(mms):
                            S.op("pe", lambda e, o_=o_, l_=l_, r_=r_, n_=n_, nm=len(mms): e.matmul(o_, lhsT=l_, rhs=r_, start=(n_ == 0), stop=(n_ == nm - 1)),
                                 reads=[l_, r_], writes=[o_])

                def post_a(i):
                    psO = self.ps[2 + i % 2]
                    S.op("act", lambda e: e.activation(out=osq[:], in_=psO[:], func=AF.Square), reads=[psO[:]], writes=[osq[:]])

                def post_b(i):
                    psR = self.ps[6 + i % 2]
                    S.op("pe", lambda e: e.matmul(psR[:], lhsT=self.ones_bf[:], rhs=osq[:], start=True, stop=True), reads=[osq[:]], writes=[psR[:]])

                def post_c(i):
                    psR = self.ps[6 + i % 2]
                    rflat = flat(rstd)
                    S.op("act", lambda e: e.activation(out=rflat, in_=psR[:], func=AF.Ln, scale=1.0 / 128, bias=self.eps_t[:, 0:1]),
                         reads=[psR[:]], writes=[rstd[:]])
                    S.op("act", lambda e: e.activation(out=rflat, in_=rflat, func=AF.Exp, scale=-0.5), reads=[rstd[:]], writes=[rstd[:]])

                def post_d(i, hf=hf):
                    tl = slice(i * 128, (i + 1) * 128)
                    psO = self.ps[2 + i % 2]
                    S.op("dve", lambda e: e.tensor_tensor(out=z[:, hf * 4:(hf + 1) * 4, tl], in0=psO[:].rearrange("p (a b) -> p a b", a=4),
                                                          in1=rstd[:], op=ALU.mult),
                         reads=[psO[:], rstd[:]], writes=[z[:, hf * 4:(hf + 1) * 4, tl]])

                pre(0)
                for i in range(17):
                    if i + 1 < 16:
                        pre(i + 1)
                    if i > 0:
                        post_a(i - 1)
                    if i < 16:
                        stateP(i, 0)
                        stateP(i, 1)
                    if i > 0:
                        post_b(i - 1)
                    if i < 16:
                        chain(i, 0)
                    if i > 0:
                        post_c(i - 1)
                    if i < 16:
                        chain(i, 1)
                        omm(i)
                    if i > 0:
                        post_d(i - 1)
                if os.environ.get("HG_DBG") == "7" and hf == 1:
                    ov = self.outT.rearrange("(c p) t -> p c t", p=128)
                    S.dma("pool", "dbg", [(ov[:, :, :], z[:, :, :])], reads=[z[:]])
                    S.wait_all_dma("pool", ["dbg"])
                    self.dbg_stop = True
                    return
                if os.environ.get("HG_DBG") == "4" and hf == 1:
                    ov = self.outT.rearrange("(c p) t -> p c t", p=128)
                    S.dma("pool", "dbg", [(ov[:, 0:4, :], z[:, 4:8, :])], reads=[z[:]])
                    S.wait_all_dma("pool", ["dbg"])
                    self.dbg_stop = True
                    return
                if os.environ.get("HG_DBG") == "3":
                    ov = self.outT.rearrange("(c p) t -> p c t", p=128)
                    S.dma("pool", "dbg", [(ov[:, 0:4, :], z[:, 0:4, :])], reads=[z[:]])
                    S.wait_all_dma("pool", ["dbg"])
                    self.dbg_stop = True
                    return

            _recur(hf, (Z_OFF + 16384) if hf == 0 else WS_OFF)
        if stop <= 3:
            return
        load(0, 6)
        load(1, 7)
        wo = self.carve(V_OFF, (128, NCH, D), BF16)
        S.dma("pool", "hwo", [(wo[:], self.w[f"hgrn_w_o_{li}"].rearrange("(k p) f -> p k f", p=128))], writes=[wo[:]])
        sgb = [self.carve(QT_OFF + i * 1024, (128, TB), BF16) for i in range(4)]
        ix = 0
        for gh in range(NCH):
            for tb in range(NTB):
                sl = slice(tb * TB, (tb + 1) * TB)
                ps = self.ps[ix % 8]
                for k in range(NCH):
                    S.op("pe", lambda e, ps=ps, k=k, gh=gh, sl=sl: e.matmul(
                            ps[:], lhsT=wsl[gh // 4][:, k, (gh % 4) * 128:(gh % 4 + 1) * 128], rhs=self.xn[:, k, sl],
                            start=(k == 0), stop=(k == NCH - 1)),
                         reads=[wsl[gh // 4][:, k, (gh % 4) * 128:(gh % 4 + 1) * 128], self.xn[:, k, sl]], writes=[ps[:]])
                sg = sgb[ix % 4]
                S.op("act", lambda e, ps=ps, sg=sg: e.activation(out=sg[:], in_=ps[:], func=AF.Silu), reads=[ps[:]], writes=[sg[:]])
                if os.environ.get("HG_DBG") == "6":
                    S.op("dve", lambda e, sg=sg, gh=gh, sl=sl: e.tensor_copy(out=z[:, gh, sl], in_=sg[:]), reads=[sg[:]], writes=[z[:, gh, sl]])
                    ix += 1
                    continue
                if os.environ.get("HG_Z3", "1") == "1":
                    S.op("dve", lambda e, sg=sg, gh=gh, sl=sl: e.scalar_tensor_tensor(out=z[:, gh, sl], in0=sg[:], scalar=pvc("hng", gh), in1=z[:, gh, sl],
                                                                                       op0=ALU.mult, op1=ALU.mult),
                         reads=[z[:, gh, sl], sg[:]], writes=[z[:, gh, sl]])
                else:
                    S.op("dve", lambda e, sg=sg, gh=gh, sl=sl: e.tensor_tensor(out=z[:, gh, sl], in0=z[:, gh, sl], in1=sg[:], op=ALU.mult),
                         reads=[z[:, gh, sl], sg[:]], writes=[z[:, gh, sl]])
                ix += 1
        if os.environ.get("HG_DBG") in ("5", "6"):
            ov = self.outT.rearrange("(c p) t -> p c t", p=128)
            S.dma("pool", "dbg", [(ov[:, :, :], z[:, :, :])], reads=[z[:]])
            S.wait_all_dma("pool", ["dbg"])
            self.dbg_stop = True
            return
        for tb in range(NTB):
            for m in range(NCH):
                sl = slice(tb * TB, (tb + 1) * TB)
                ps = self.ps[(tb * NCH + m) % 8]
                for j in range(NCH):
                    S.op("pe", lambda e, ps=ps, m=m, j=j, sl=sl: e.matmul(
                            ps[:], lhsT=wo[:, j, m * 128:(m + 1) * 128], rhs=z[:, j, sl], start=(j == 0), stop=(j == NCH - 1)),
                         reads=[wo[:, j, m * 128:(m + 1) * 128], z[:, j, sl]], writes=[ps[:]])
                S.op("dve", lambda e, ps=ps, m=m, sl=sl: e.tensor_tensor(out=self.h[:, m, sl], in0=self.h[:, m, sl], in1=ps[:], op=ALU.add),
                     reads=[self.h[:, m, sl], ps[:]], writes=[self.h[:, m, sl]])

    def mlp(self, li):
        S = self.S
        G = 8
        NG = DFF // (G * 128)
        HID_OFF = 0
        W_OFF = 32768
        hid = self.carve(HID_OFF, (128, G, T), BF16)
        w1s = [self.carve(W_OFF + s * 32768, (128, NCH, G * 128), BF16) for s in range(2)]
        w2s = [self.carve(W_OFF + s * 32768 + 16384, (128, G, D), BF16) for s in range(2)]
        w1 = self.w[f"mlp_w1_{li}"].rearrange("(k p) f -> p k f", p=128)
        w2 = self.w[f"mlp_w2_{li}"].rearrange("(c p) m -> p c m", p=128)

        def load(g):
            s = g % 2
            S.dma("pool", f"mw1_{s}", [(w1s[s][:], w1[:, :, g * G * 128:(g + 1) * G * 128])], writes=[w1s[s][:]])
            S.dma("pool", f"mw2_{s}", [(w2s[s][:], w2[:, g * G:(g + 1) * G, :])], writes=[w2s[s][:]])

        load(0)
        self.norm_to_xn(4 + li, scratch_off=HID_OFF)
        for g in range(NG):
            if g + 1 < NG:
                load(g + 1)
            s = g % 2
            for c in range(G):
                for tb in range(NTB):
                    sl = slice(tb * TB, (tb + 1) * TB)
                    ps = self.psbank()
                    for k in range(NCH):
                        S.op("pe", lambda e, ps=ps, s=s, k=k, c=c, sl=sl: e.matmul(
                                ps[:], lhsT=w1s[s][:, k, c * 128:(c + 1) * 128], rhs=self.xn[:, k, sl],
                                start=(k == 0), stop=(k == NCH - 1)),
                             reads=[w1s[s][:, k, c * 128:(c + 1) * 128], self.xn[:, k, sl]], writes=[ps[:]])
                    S.op("act", lambda e, ps=ps, c=c, sl=sl: e.activation(out=hid[:, c, sl], in_=ps[:], func=AF.Relu),
                         reads=[ps[:]], writes=[hid[:, c, sl]])
                    S.op("pool", lambda e, c=c, sl=sl: e.tensor_tensor(out=hid[:, c, sl], in0=hid[:, c, sl], in1=hid[:, c, sl], op=ALU.mult),
                         reads=[hid[:, c, sl]], writes=[hid[:, c, sl]])
            for tb in range(NTB):
                for m in range(NCH):
                    sl = slice(tb * TB, (tb + 1) * TB)
                    ps = self.psbank()
                    for c in range(G):
                        S.op("pe", lambda e, ps=ps, s=s, m=m, c=c, sl=sl: e.matmul(
                                ps[:], lhsT=w2s[s][:, c, m * 128:(m + 1) * 128], rhs=hid[:, c, sl],
                                start=(c == 0), stop=(c == G - 1)),
                             reads=[w2s[s][:, c, m * 128:(m + 1) * 128], hid[:, c, sl]], writes=[ps[:]])
                    S.op("dve", lambda e, ps=ps, m=m, sl=sl: e.tensor_tensor(out=self.h[:, m, sl], in0=self.h[:, m, sl], in1=ps[:], op=ALU.add),
                         reads=[self.h[:, m, sl], ps[:]], writes=[self.h[:, m, sl]])

    def build(self):
        mixers, mlps = self.do_mixers, self.do_mlps
        self.declare()
        self.eps_t = self.sb("eps_t", (128, 1), F32)
        self.es_t = self.sb("es_t", (128, 2 * NCH), F32)
        self.ident_bf = self.sb("ident_bf", (128, 128), BF16)
        self.iota_i = self.sb("iota_i", (128, 256), mybir.dt.int32)
        idi = self.iota_i[:, 0:128]
        self.S.op("pool", lambda e: e.iota(idi[:], pattern=[[1, 128]], base=0, channel_multiplier=-1), writes=[idi[:]])
        self.S.op("dve", lambda e: e.tensor_single_scalar(out=self.ident_bf[:], in_=idi[:], scalar=0, op=ALU.is_equal),
                  reads=[idi[:]], writes=[self.ident_bf[:]])
        self.S.op("dve", lambda e: e.memset(self.eps_t[:], EPS), writes=[self.eps_t[:]])
        self.setup()
        for li in self.layers:
            if mixers:
                [self.attention, self.conv, self.hgrn][LAYER_KIND[li]](li)
            if mlps:
                self.mlp(li)
        if getattr(self, "dbg_stop", False):
            pass
        elif self.final_norm:
            self.final()
        else:
            self.store_h()
        self.S.emit(self.nc, self.es)
        self.es.close()
        return self.nc


def _pc(v):
    v = np.asarray(v, dtype=np.float32).reshape(-1, NCH, 128)
    return np.ascontiguousarray(v.transpose(2, 0, 1).reshape(128, -1))


def _in_maps(inputs, layers=(0, 1, 2, 3), mixers=True, mlps=True):
    x = np.asarray(inputs["x"], dtype=np.float32)
    shared = {}
    for li in layers:
        j = li // 3
        if mlps:
            shared[f"mlp_w1_{li}"] = np.ascontiguousarray(np.asarray(inputs["mlp_w1"][li], dtype=np.float32))
            shared[f"mlp_w2_{li}"] = np.ascontiguousarray(np.asarray(inputs["mlp_w2"][li], dtype=np.float32))
        if mixers:
            for nm, shp in MIXER_W[LAYER_KIND[li]]:
                shared[f"{nm}_{li}"] = np.ascontiguousarray(np.asarray(inputs[nm][j], dtype=np.float32))
    pv = np.zeros((128, PV_N), np.float32)
    gains = np.concatenate([np.asarray(inputs["norm_mixer"], np.float32), np.asarray(inputs["norm_mlp"], np.float32),
                            np.asarray(inputs["final_norm"], np.float32).reshape(1, D)], axis=0)
    pv[:, PV["gains"][0]:PV["gains"][1]] = _pc(gains)
    sk = np.asarray(inputs["attn_sinks"], np.float32).reshape(2, NCH, 2)
    pv[:, PV["sinks"][0]:PV["sinks"][1]] = np.repeat(sk.transpose(2, 0, 1).reshape(2, 2 * NCH), 64, axis=0)
    for nm, key in (("cb1", "conv_b_pw1"), ("cdw", "conv_w_dw"), ("cbd", "conv_b_dw"), ("clg", "conv_ln_g"),
                    ("clb", "conv_ln_b"), ("cb2", "conv_b_pw2"), ("hlb", "hgrn_lower_bounds"), ("hng", "hgrn_norm_g")):
        pv[:, PV[nm][0]:PV[nm][1]] = _pc(inputs[key])
    shared["pvec"] = pv
    maps = []
    for b in range(N_CORES):
        m = dict(shared)
        m["xT"] = np.ascontiguousarray(x[b].T)
        maps.append(m)
    return maps


def run(inputs, **bkw):
    b = Builder(**bkw)
    nc = b.build()
    maps = _in_maps(inputs, b.layers, b.do_mixers, b.do_mlps)
    res = run_bass_kernel_spmd(nc, maps, core_ids=list(range(N_CORES)))
    out = np.stack([np.asarray(r["outT"]).T for r in res.results], axis=0)
    return np.ascontiguousarray(out.astype(np.float32))


def kernel(**inputs):
    return run(inputs)
```
